# Optimizing a Trainium2 kernel written in Bass

```python
import math
import jax, jax.numpy as jnp
from jax import lax
import numpy as np

D_MODEL = 1024
BATCH = 1
SEQ = 16384
DEPTH = 1

CHUNK = 64
MIX_WIDTH = D_MODEL
ATTN_WIDTH = MIX_WIDTH // 2
LRU_WIDTH = MIX_WIDTH - ATTN_WIDTH
ATTN_HEAD_DIM = 64
ATTN_HEADS = ATTN_WIDTH // (2 * ATTN_HEAD_DIM)
LRU_BLOCKS = 8
LRU_BLOCK_DIM = LRU_WIDTH // LRU_BLOCKS
CONV_WIDTH = 4
LRU_C = 8.0
Q_BLOCK = 128
IN_COLS = 3 * ATTN_WIDTH + 2 * LRU_WIDTH
PEER_HEADS = 8
PEER_N_KEYS = 128
PEER_N_EXPERTS = PEER_N_KEYS * PEER_N_KEYS
PEER_QUERY_DIM = 256
PEER_HALF = PEER_QUERY_DIM // 2
PEER_TOPK = 16
PEER_BLOCK = 128
EPS = 1e-6

kernel_name = "hybrid_diffattn_rglru_peer"


def rmsnorm(x, g):
    x32 = x.astype(jnp.float32)
    y = x32 * lax.rsqrt(jnp.mean(x32 * x32, axis=-1, keepdims=True) + EPS)
    return (y * g.astype(jnp.float32)).astype(x.dtype)


def diff_attention(q, k, v, lam, subln_g, lambda_init):
    B, S = q.shape[0], q.shape[1]
    nb = S // Q_BLOCK
    scale = ATTN_HEAD_DIM ** -0.5
    k32 = k.astype(jnp.float32)
    v32 = v.astype(jnp.float32)
    k_chunk = jnp.arange(S) // CHUNK
    qb = q.reshape(B, nb, Q_BLOCK, ATTN_HEADS, 2, ATTN_HEAD_DIM).swapaxes(0, 1)

    def block(args):
        qblk, i = args
        q_chunk = (i * Q_BLOCK + jnp.arange(Q_BLOCK)) // CHUNK
        s = jnp.einsum('bqhmd,bkhmd->bhmqk', qblk.astype(jnp.float32), k32) * scale
        mask = q_chunk[:, None] >= k_chunk[None, :]
        s = jnp.where(mask, s, -1e30)
        p = jax.nn.softmax(s, axis=-1)
        p = p[:, :, 0] - lam * p[:, :, 1]
        return jnp.einsum('bhqk,bkhe->bqhe', p, v32)

    o = lax.map(block, (qb, jnp.arange(nb)))
    o = o.swapaxes(0, 1).reshape(B, S, ATTN_HEADS, 2 * ATTN_HEAD_DIM)
    o32 = o * lax.rsqrt(jnp.mean(o * o, axis=-1, keepdims=True) + EPS)
    o32 = o32 * subln_g.astype(jnp.float32) * (1.0 - lambda_init)
    return o32.reshape(B, S, ATTN_WIDTH).astype(q.dtype)


def rglru_branch(xr, gate, conv_w, conv_b, w_a, b_a, w_x, b_x, lru_lambda):
    B, S, W = xr.shape
    xp = jnp.pad(xr, ((0, 0), (CONV_WIDTH - 1, 0), (0, 0)))
    y = conv_b
    for t in range(CONV_WIDTH):
        y = y + xp[:, t:t + S, :] * conv_w[t]
    yb = y.reshape(B, S, LRU_BLOCKS, LRU_BLOCK_DIM)
    r = jax.nn.sigmoid((jnp.einsum('bsnd,nde->bsne', yb, w_a).reshape(B, S, W) + b_a).astype(jnp.float32))
    i = jax.nn.sigmoid((jnp.einsum('bsnd,nde->bsne', yb, w_x).reshape(B, S, W) + b_x).astype(jnp.float32))
    log_a = -LRU_C * r * jax.nn.softplus(-lru_lambda.astype(jnp.float32))
    a = jnp.exp(log_a)
    mult = jnp.sqrt(jnp.maximum(-jnp.expm1(2.0 * log_a), 1e-12))
    b = mult * (i * y.astype(jnp.float32))

    def combine(left, right):
        a1, b1 = left
        a2, b2 = right
        return a1 * a2, a2 * b1 + b2

    _, h = lax.associative_scan(combine, (a, b), axis=1)
    out = h * jax.nn.gelu(gate.astype(jnp.float32))
    return out.astype(xr.dtype)


def peer_ffn(xn, w_query, sub_keys_1, sub_keys_2, expert_down, expert_up):
    B, S, D = xn.shape
    T = B * S
    xt = xn.reshape(T, D)
    q = (xt @ w_query).reshape(T, PEER_HEADS, 2, PEER_HALF).astype(jnp.float32)
    s1 = jnp.einsum('thd,kd->thk', q[:, :, 0], sub_keys_1.astype(jnp.float32))
    s2 = jnp.einsum('thd,kd->thk', q[:, :, 1], sub_keys_2.astype(jnp.float32))
    v1, i1 = lax.top_k(s1, PEER_TOPK)
    v2, i2 = lax.top_k(s2, PEER_TOPK)
    cand = (v1[..., :, None] + v2[..., None, :]).reshape(T, PEER_HEADS, PEER_TOPK * PEER_TOPK)
    sc, ci = lax.top_k(cand, PEER_TOPK)
    eid = (jnp.take_along_axis(i1, ci // PEER_TOPK, axis=-1) * PEER_N_KEYS
           + jnp.take_along_axis(i2, ci % PEER_TOPK, axis=-1))
    g = jax.nn.softmax(sc, axis=-1)
    nbt = T // PEER_BLOCK

    def block(args):
        xb, eb, gb = args
        u = expert_down[eb]
        act = jax.nn.gelu(jnp.einsum('thkd,td->thk', u, xb).astype(jnp.float32))
        w = (gb * act).astype(xb.dtype)
        return jnp.einsum('thk,thkd->td', w, expert_up[eb])

    out = lax.map(block, (xt.reshape(nbt, PEER_BLOCK, D),
                          eid.reshape(nbt, PEER_BLOCK, PEER_HEADS, PEER_TOPK),
                          g.reshape(nbt, PEER_BLOCK, PEER_HEADS, PEER_TOPK)))
    return out.reshape(B, S, D).astype(xn.dtype)


def setup_inputs(seed: int = 0) -> dict:
    key = jax.random.key(seed)
    ks = jax.random.split(key, 24)
    f32 = jnp.float32
    L, D = DEPTH, D_MODEL
    nrm = lambda k, shape, s: jax.random.normal(k, shape, f32) * s
    a8 = jax.random.uniform(ks[14], (L, LRU_WIDTH), f32, 0.9, 0.999)
    a = a8 ** (1.0 / LRU_C)
    return {
        "x": jax.random.normal(ks[0], (BATCH, SEQ, D), f32),
        "norm1_g": 1.0 + nrm(ks[1], (L, D), 0.02),
        "w_in": nrm(ks[2], (L, D, IN_COLS), D ** -0.5),
        "lambda_q1": nrm(ks[3], (L, ATTN_HEAD_DIM), 0.1),
        "lambda_k1": nrm(ks[4], (L, ATTN_HEAD_DIM), 0.1),
        "lambda_q2": nrm(ks[5], (L, ATTN_HEAD_DIM), 0.1),
        "lambda_k2": nrm(ks[6], (L, ATTN_HEAD_DIM), 0.1),
        "subln_g": 1.0 + nrm(ks[7], (L, 2 * ATTN_HEAD_DIM), 0.02),
        "conv_w": nrm(ks[8], (L, CONV_WIDTH, LRU_WIDTH), CONV_WIDTH ** -0.5),
        "conv_b": nrm(ks[9], (L, LRU_WIDTH), 0.01),
        "w_rec_gate": nrm(ks[10], (L, LRU_BLOCKS, LRU_BLOCK_DIM, LRU_BLOCK_DIM), LRU_BLOCK_DIM ** -0.5),
        "b_rec_gate": nrm(ks[11], (L, LRU_WIDTH), 0.01),
        "w_in_gate": nrm(ks[12], (L, LRU_BLOCKS, LRU_BLOCK_DIM, LRU_BLOCK_DIM), LRU_BLOCK_DIM ** -0.5),
        "b_in_gate": nrm(ks[13], (L, LRU_WIDTH), 0.01),
        "lru_lambda": jnp.log(a) - jnp.log1p(-a),
        "w_out": nrm(ks[15], (L, MIX_WIDTH, D), MIX_WIDTH ** -0.5),
        "norm2_g": 1.0 + nrm(ks[16], (L, D), 0.02),
        "w_query": nrm(ks[17], (L, D, PEER_HEADS * PEER_QUERY_DIM), D ** -0.5),
        "sub_keys_1": nrm(ks[18], (L, PEER_N_KEYS, PEER_HALF), PEER_HALF ** -0.5),
        "sub_keys_2": nrm(ks[19], (L, PEER_N_KEYS, PEER_HALF), PEER_HALF ** -0.5),
        "expert_down": nrm(ks[20], (L, PEER_N_EXPERTS, D), D ** -0.5),
        "expert_up": nrm(ks[21], (L, PEER_N_EXPERTS, D), (PEER_HEADS * PEER_TOPK) ** -0.5),
        "norm_f_g": 1.0 + nrm(ks[22], (D,), 0.02),
    }


def reference(x, norm1_g, w_in, lambda_q1, lambda_k1, lambda_q2, lambda_k2, subln_g,
              conv_w, conv_b, w_rec_gate, b_rec_gate, w_in_gate, b_in_gate, lru_lambda,
              w_out, norm2_g, w_query, sub_keys_1, sub_keys_2, expert_down, expert_up,
              norm_f_g):
    B, S, D = x.shape
    h = x
    for l in range(DEPTH):
        lambda_init = 0.8 - 0.6 * math.exp(-0.3 * l)
        n = rmsnorm(h, norm1_g[l])
        p = n @ w_in[l]
        q = p[..., :ATTN_WIDTH].reshape(B, S, ATTN_HEADS, 2, ATTN_HEAD_DIM)
        k = p[..., ATTN_WIDTH:2 * ATTN_WIDTH].reshape(B, S, ATTN_HEADS, 2, ATTN_HEAD_DIM)
        v = p[..., 2 * ATTN_WIDTH:3 * ATTN_WIDTH].reshape(B, S, ATTN_HEADS, 2 * ATTN_HEAD_DIM)
        xr = p[..., 3 * ATTN_WIDTH:3 * ATTN_WIDTH + LRU_WIDTH]
        gate = p[..., 3 * ATTN_WIDTH + LRU_WIDTH:]
        lam = (jnp.exp(jnp.sum(lambda_q1[l].astype(jnp.float32) * lambda_k1[l].astype(jnp.float32)))
               - jnp.exp(jnp.sum(lambda_q2[l].astype(jnp.float32) * lambda_k2[l].astype(jnp.float32)))
               + lambda_init)
        attn_out = diff_attention(q, k, v, lam, subln_g[l], lambda_init)
        lru_out = rglru_branch(xr, gate, conv_w[l], conv_b[l], w_rec_gate[l], b_rec_gate[l],
                               w_in_gate[l], b_in_gate[l], lru_lambda[l])
        mix = jnp.concatenate([attn_out, lru_out], axis=-1)
        h = h + mix @ w_out[l]
        h = h + peer_ffn(rmsnorm(h, norm2_g[l]), w_query[l], sub_keys_1[l], sub_keys_2[l],
                         expert_down[l], expert_up[l])
    return rmsnorm(h, norm_f_g)
```

```python
import numpy as np
from contextlib import ExitStack
import concourse.bass as bass
import concourse.mybir as mybir
from concourse.bass_utils import run_bass_kernel_spmd

F32 = mybir.dt.float32
BF16 = mybir.dt.bfloat16
U32 = mybir.dt.uint32
AF = mybir.ActivationFunctionType
ALU = mybir.AluOpType

D = 1024
NCORES = 8
EPS = 1e-6
EPOCH = 12000
NEXP = 16384
GELU_C = 1.5957691216057308


class Buf:
    __slots__ = ("name", "w", "r", "dsem", "dcnt", "ew")

    def __init__(self, name):
        self.name = name
        self.ew = None
        self.w = None
        self.r = []
        self.dsem = None
        self.dcnt = 0


class Sched:
    COMPUTE = ("pe", "act", "dve", "pool")

    def __init__(self, nc, es):
        self.nc = nc
        self.es = es
        self.eng = {"pe": nc.tensor, "act": nc.scalar, "dve": nc.vector, "pool": nc.gpsimd, "sp": nc.sync}
        self.cnt = {e: 0 for e in self.eng}
        self.esems = {e: [] for e in self.eng}
        self.known = {e: {} for e in self.eng}
        self.dsems = {}
        self.dbufs = []
        self.nsem = 0
        self.ninst = {e: 0 for e in self.eng}
        self.bg = set()

    def _newsem(self, name):
        self.nsem += 1
        return self.es.enter_context(self.nc.semaphore(name))

    def _etok_sem(self, e, idx):
        ep = (idx - 1) // EPOCH
        while len(self.esems[e]) <= ep:
            self.esems[e].append(self._newsem(f"c_{e}_{len(self.esems[e])}"))
        return self.esems[e][ep], (idx - 1) % EPOCH + 1, (e, ep)

    def _wait(self, e, tok):
        if tok is None:
            return
        if tok[0] == "e":
            sem, val, key = self._etok_sem(tok[1], tok[2])
        else:
            sem, val, key = self.dsems[tok[1]], tok[2], tok[1]
        if self.known[e].get(key, 0) >= val:
            return
        self.known[e][key] = val
        self.eng[e].wait_ge(sem, val)
        self.ninst[e] += 1

    STRICT = True

    def deps(self, e, reads, writes):
        for b in reads:
            self._wait(e, b.w)
        for b in writes:
            strict = self.STRICT and e != "pe"
            if b.w is not None and (strict or not (b.w[0] == "e" and b.w[1] == e)):
                self._wait(e, b.w)
            for t in b.r:
                if t[0] == "e" and t[1] == e:
                    continue
                self._wait(e, t)

    def _mark(self, tok, reads, writes):
        for b in reads:
            b.r.append(tok)
            if len(b.r) > 48:
                last = {}
                for t in b.r:
                    last[(t[0], t[1])] = t
                b.r = list(last.values())
        for b in writes:
            b.w = tok
            b.ew = tok
            b.r = []

    def op(self, e, fn, reads=(), writes=()):
        self.deps(e, reads, writes)
        inst = fn(self.eng[e])
        self.cnt[e] += 1
        idx = self.cnt[e]
        sem, val, key = self._etok_sem(e, idx)
        inst.then_inc(sem, 1)
        self.ninst[e] += 1
        tok = ("e", e, idx)
        self._mark(tok, reads, writes)
        return tok

    def dma(self, q, out, in_, reads=(), writes=(), acc=False, fn=None, **kw):
        wb = writes[0]
        if acc:
            for b in reads:
                self._wait(q, b.w)
            for b in writes:
                if b.w is not None and b.w[0] == "e":
                    self._wait(q, b.w)
                elif b.ew is not None:
                    self._wait(q, b.ew)
                for t in b.r:
                    self._wait(q, t)
        else:
            self.deps(q, reads, writes)
        if wb.dsem is None:
            wb.dsem = "d_" + wb.name
            self.dsems[wb.dsem] = self._newsem(wb.dsem)
            self.dbufs.append(wb)
        wb.dcnt += 16
        sem = self.dsems[wb.dsem]
        if fn is not None:
            inst = fn(self.eng[q])
        else:
            inst = self.eng[q].dma_start(out=out, in_=in_, **kw)
        inst.then_inc(sem, 16)
        self.ninst[q] += 1
        tok = ("d", wb.dsem, wb.dcnt)
        for b in reads:
            b.r.append(tok)
            if len(b.r) > 48:
                last = {}
                for t in b.r:
                    last[(t[0], t[1])] = t
                b.r = list(last.values())
        for b in writes:
            b.w = tok
            if not acc:
                b.r = []
        return tok

    def barrier(self):
        toks = []
        for e in self.COMPUTE:
            if self.cnt[e] > 0:
                toks.append(("e", e, self.cnt[e]))
        for b in self.dbufs:
            if b in self.bg:
                continue
            toks.append(("d", b.dsem, b.dcnt))
        for e in self.eng:
            for t in toks:
                self._wait(e, t)

    def final_wait(self, e, bufs):
        for b in bufs:
            self._wait(e, b.w)


PV_G1 = 0
PV_CW = 8
PV_CB = 24
PV_BA = 28
PV_BX = 32
PV_LAM = 36
PV_N = 40
BV_G2 = 0
BV_GF = 1024
BV_SG = 2048
BV_LQ1 = 2176
BV_LK1 = 2240
BV_LQ2 = 2304
BV_LK2 = 2368
BV_N = 2432


def build(S_LEN, debug=False, stop_after=None):
    NB = S_LEN // 128
    NS = NB // NCORES
    G = S_LEN // 512
    assert NB % 8 == 0 and G * 4 == NB and G % 2 == 0
    nc = bass.Bass("TRN2", target_bir_lowering=False)
    top = ExitStack()
    S = Sched(nc, top)

    def dram(name, shape, dt, kind="ExternalInput"):
        return nc.dram_tensor(name, shape, dt, kind=kind).ap()

    x_full = dram("x_full", [S_LEN, D], F32)
    x_own = dram("x_own", [NS * 128, D], F32)
    w_in = dram("w_in", [D, 2560], F32)
    w_out = dram("w_out", [D, D], F32)
    w_query = dram("w_query", [D, 2048], F32)
    sk = dram("sk", [2, 128, 128], F32)
    wgate = dram("wgate", [2, 8, 64, 64], F32)
    e_down = dram("expert_down", [NEXP, D], F32)
    e_up = dram("expert_up", [NEXP, D], F32)
    pvec_d = dram("pvec", [128, PV_N], F32)
    bvec_d = dram("bvec", [1, BV_N], F32)
    tmask_d = dram("tmask", [128, 8, 128], F32)
    selw_d = dram("selw", [128, 8], F32)
    out_d = dram("out_own", [NS * 128, D], F32, kind="ExternalOutput")
    skind = "ExternalOutput" if debug else "Internal"
    kT_d = dram("kT_scr", [4, 128, S_LEN], BF16, kind=skind)
    v_d = dram("v_scr", [4, 128, NB, 128], BF16, kind=skind)
    dbg = {}
    if debug:
        dbg["qT"] = dram("dbg_qT", [128, 4, NS * 128], BF16, kind="ExternalOutput")
        dbg["lru"] = dram("dbg_lru", [128, 4, NS * 128], BF16, kind="ExternalOutput")
        dbg["attn"] = dram("dbg_attn", [128, 4, NS * 128], BF16, kind="ExternalOutput")
        dbg["h1"] = dram("dbg_h1", [NS * 128, D], F32, kind="ExternalOutput")
        dbg["eid"] = dram("dbg_eid", [NS * 128, 128], U32, kind="ExternalOutput")
        dbg["g"] = dram("dbg_g", [NS * 128, 128], F32, kind="ExternalOutput")
        dbg["act"] = dram("dbg_act", [NS * 128, 128], F32, kind="ExternalOutput")
    B_dbg = Buf("dbgout")
    B_scr = {"kT": Buf("kTscr"), "v": Buf("vscr")}
    e_cat = dram("e_cat_bf", [NEXP, 2 * D], BF16, kind="Internal")
    B_tabs = Buf("etabs")
    S.bg.add(B_tabs)

    CR = 512
    conv_jobs = [(half, r) for r in range(NEXP // CR) for half in range(2)]

    def table_conversion_step(n):
        for _ in range(n):
            if not conv_jobs:
                return
            half, r = conv_jobs.pop(0)
            src_t = (e_down, e_up)[half]
            S.dma("pool", e_cat[r * CR:(r + 1) * CR, half * D:(half + 1) * D], src_t[r * CR:(r + 1) * CR, :],
                  writes=[B_tabs], acc=True)

    def sb(es, name, shape, dt):
        return es.enter_context(nc.sbuf_tensor(name, shape, dt)), Buf(name)

    def ps(es, name, shape, dt):
        return es.enter_context(nc.psum_tensor(name, shape, dt)), Buf(name)

    ident, B_ident = sb(top, "ident", [128, 128], BF16)
    pvec, B_pvec = sb(top, "pvec_sb", [128, PV_N], F32)
    bvec, B_bvec = sb(top, "bvec_sb", [128, BV_N], F32)
    tmask, B_tmask = sb(top, "tmask_bf", [128, 8, 128], BF16)
    selw, B_selw = sb(top, "selw_sb", [128, 8], F32)
    consts, B_consts = sb(top, "consts", [128, 16], F32)
    mixT_lru, B_mlru = sb(top, "mixT_lru", [128, 4, NS * 128], BF16)
    mixT_attn, B_mattn = sb(top, "mixT_attn", [128, 4, NS * 128], BF16)
    sgl, B_sgl = sb(top, "sgl", [128, 128], F32)
    pq = ExitStack()
    qT_own, B_qT = sb(pq, "qT_own", [128, 4, NS * 128], BF16)
    NEGH = consts[:, 0:1]

    def rstd_from_ss(ss_ap, B_ss, n, out_ap, B_out, tmp_ap, B_tmp):
        S.op("dve", lambda e: e.tensor_scalar(out=tmp_ap, in0=ss_ap, scalar1=1.0 / n, scalar2=EPS,
                                              op0=ALU.mult, op1=ALU.add), [B_ss], [B_tmp])
        k = tmp_ap.shape[1] if len(tmp_ap.shape) > 1 else 1
        nh = NEGH if k == 1 else NEGH.broadcast_to([128, k])
        S.op("pool", lambda e: e.tensor_tensor(out=out_ap, in0=tmp_ap, in1=nh, op=ALU.pow),
             [B_tmp, B_consts], [B_out])

    with ExitStack() as p0:
        ioc, B_ioc = sb(p0, "ioc", [128, 128], F32)
        iop, B_iop = sb(p0, "iop", [128, 128], F32)
        tmf, B_tmf = sb(p0, "tmf", [128, 8, 128], F32)
        sc0, B_sc0 = sb(p0, "sc0", [128, 64], F32)
        sm0, B_sm0 = sb(p0, "sm0", [128, 8], F32)
        S.dma("sp", pvec[:], pvec_d[:, :], writes=[B_pvec])
        S.dma("sp", bvec[:], bvec_d[0, :].partition_broadcast(128), writes=[B_bvec])
        S.dma("sp", tmf[:], tmask_d[:, :, :], writes=[B_tmf])
        S.dma("sp", selw[:], selw_d[:, :], writes=[B_selw])
        S.op("pool", lambda e: e.iota(ioc[:], pattern=[[1, 128]], base=0, channel_multiplier=0,
                                      allow_small_or_imprecise_dtypes=True), [], [B_ioc])
        S.op("pool", lambda e: e.iota(iop[:], pattern=[[0, 128]], base=0, channel_multiplier=1,
                                      allow_small_or_imprecise_dtypes=True), [], [B_iop])
        S.op("dve", lambda e: e.tensor_tensor(out=ident[:], in0=ioc[:], in1=iop[:], op=ALU.is_equal),
             [B_ioc, B_iop], [B_ident])
        S.op("dve", lambda e: e.tensor_copy(out=tmask[:], in_=tmf[:]), [B_tmf], [B_tmask])
        S.op("pool", lambda e: e.memset(consts[:, 0:1], -0.5), [], [B_consts])
        S.op("act", lambda e: e.activation(out=sm0[:, 0:4], in_=pvec[:, PV_LAM:PV_LAM + 4], func=AF.Exp, scale=-1.0),
             [B_pvec], [B_sm0])
        S.op("act", lambda e: e.activation(out=sm0[:, 0:4], in_=sm0[:, 0:4], func=AF.Ln, bias=1.0),
             [B_sm0], [B_sm0])
        S.op("dve", lambda e: e.tensor_scalar(out=consts[:, 1:5], in0=sm0[:, 0:4], scalar1=-8.0, scalar2=None,
                                              op0=ALU.mult), [B_sm0], [B_consts])
        S.op("dve", lambda e: e.scalar_tensor_tensor(out=sc0[:], in0=bvec[:, BV_LQ1:BV_LQ1 + 64], scalar=1.0,
                                                     in1=bvec[:, BV_LK1:BV_LK1 + 64],
                                                     op0=ALU.mult, op1=ALU.mult, accum_out=sm0[:, 4:5]),
             [B_bvec], [B_sc0, B_sm0])
        S.op("dve", lambda e: e.scalar_tensor_tensor(out=sc0[:], in0=bvec[:, BV_LQ2:BV_LQ2 + 64], scalar=1.0,
                                                     in1=bvec[:, BV_LK2:BV_LK2 + 64],
                                                     op0=ALU.mult, op1=ALU.mult, accum_out=sm0[:, 5:6]),
             [B_bvec], [B_sc0, B_sm0])
        S.op("act", lambda e: e.activation(out=sm0[:, 6:8], in_=sm0[:, 4:6], func=AF.Exp), [B_sm0], [B_sm0])
        S.op("dve", lambda e: e.tensor_tensor(out=sm0[:, 4:5], in0=sm0[:, 7:8], in1=sm0[:, 6:7], op=ALU.subtract),
             [B_sm0], [B_sm0])
        S.op("dve", lambda e: e.tensor_scalar(out=consts[:, 5:6], in0=sm0[:, 4:5], scalar1=-0.2, scalar2=None,
                                              op0=ALU.add), [B_sm0], [B_consts])
        S.op("dve", lambda e: e.tensor_scalar(out=sgl[:], in0=bvec[:, BV_SG:BV_SG + 128], scalar1=0.8, scalar2=None,
                                              op0=ALU.mult), [B_bvec], [B_sgl])
        S.barrier()
    NEGLAM = consts[:, 5:6]

    def load_weight_bf(es_stage, dst, B_dst, src_ap, ncols, scale_col=None, nm="w"):
        stg = []
        for i in range(2):
            stg.append(sb(es_stage, f"stg_{nm}{i}", [128, ncols], F32))
        for k in range(8):
            t, B_t = stg[k % 2]
            S.dma("sp", t[:], src_ap[k * 128:(k + 1) * 128, :], writes=[B_t])
            if scale_col is not None:
                S.op("dve", lambda e, t=t, k=k: e.tensor_scalar(out=dst[:, k, :], in0=t[:],
                                                               scalar1=pvec[:, scale_col + k:scale_col + k + 1],
                                                               scalar2=None, op0=ALU.mult), [B_t, B_pvec], [B_dst])
            else:
                S.op("act", lambda e, t=t, k=k: e.activation(out=dst[:, k, :], in_=t[:], func=AF.Copy),
                     [B_t], [B_dst])

    def gelu_tanh(x_ap, B_x, out_ap, B_out, t1, B_t1, t2, B_t2):
        S.op("act", lambda e: e.activation(out=t1, in_=x_ap, func=AF.Square), [B_x], [B_t1])
        S.op("dve", lambda e: e.tensor_scalar(out=t1, in0=t1, scalar1=0.044715, scalar2=1.0,
                                              op0=ALU.mult, op1=ALU.add), [B_t1], [B_t1])
        S.op("dve", lambda e: e.tensor_tensor(out=t2, in0=t1, in1=x_ap, op=ALU.mult), [B_t1, B_x], [B_t2])
        S.op("act", lambda e: e.activation(out=t2, in_=t2, func=AF.Sigmoid, scale=GELU_C), [B_t2], [B_t2])
        S.op("dve", lambda e: e.tensor_tensor(out=out_ap, in0=t2, in1=x_ap, op=ALU.mult), [B_t2, B_x], [B_out])

    with ExitStack() as pa:
        w_in_bf, B_win = sb(pa, "w_in_bf", [128, 8, 2560], BF16)
        wbd, B_wbd = sb(pa, "wbd", [128, 2, 4, 128], BF16)
        NXS = 2
        xts = [sb(pa, f"xt{i}", [128, D], F32) for i in range(NXS)]
        nbf = [sb(pa, f"nbf{i}", [128, D], BF16) for i in range(2)]
        sq_junk, B_sqj = sb(pa, "sq_junk", [128, D], BF16)
        nT, B_nT = sb(pa, "nT", [128, 8, 512], BF16)
        stat, B_stat = sb(pa, "stat", [128, 8], F32)
        B_stats = [Buf("stat0"), Buf("stat1")]
        kT_sb, B_kTsb = sb(pa, "kT_sb", [128, 4, 512], BF16)
        v_sb, B_vsb = sb(pa, "v_sb", [128, 4, 4, 128], BF16)
        xrbuf, B_xr = sb(pa, "xrbuf", [128, 4, 515], F32)
        ybuf, B_y = sb(pa, "ybuf", [128, 4, 512], F32)
        ybf, B_ybf = sb(pa, "ybf", [128, 4, 512], BF16)
        ra, B_ra = sb(pa, "ra", [128, 4, 512], F32)
        ib, B_ib = sb(pa, "ib", [128, 4, 512], F32)
        hb, B_hb = sb(pa, "hb", [128, 4, 512], F32)
        carry, B_carry = sb(pa, "carry", [128, 4], F32)
        selacc, B_selacc = sb(pa, "selacc", [128, 4, 128], F32)
        tp = [ps(pa, f"tpA{i}", [128, 1024], BF16) for i in range(2)]
        mm = [ps(pa, f"mmA{i}", [128, 512], F32) for i in range(3)]
        gp = [ps(pa, f"gpA{i}", [128, 512], F32) for i in range(2)]
        mmi = [0]

        def next_mm():
            mmi[0] += 1
            return mm[mmi[0] % 3]

        with ExitStack() as pw:
            load_weight_bf(pw, w_in_bf, B_win, w_in, 2560, scale_col=PV_G1, nm="win")
            wbf, B_wbf = sb(pw, "wbf", [128, 2, 4, 128], F32)
            S.op("pool", lambda e: e.memset(wbf[:], 0.0), [], [B_wbf])
            for a in range(2):
                for ct in range(4):
                    for hf in range(2):
                        S.dma("sp", wbf[hf * 64:(hf + 1) * 64, a, ct, hf * 64:(hf + 1) * 64],
                              wgate[a, 2 * ct + hf, :, :], writes=[B_wbf], acc=True)
            S.op("dve", lambda e: e.tensor_copy(out=wbd[:], in_=wbf[:]), [B_wbf], [B_wbd])
            S.op("dve", lambda e: e.memset(xrbuf[:], 0.0), [], [B_xr])
            S.op("dve", lambda e: e.memset(carry[:], 0.0), [], [B_carry])
            S.barrier()

        def norm_tile(src_ap, slot, par, col):
            xt, B_xt = xts[slot]
            nb_, B_nb = nbf[par]
            tp_, B_tp = tp[par]
            B_st = B_stats[par]
            S.dma("sp", xt[:], src_ap, writes=[B_xt])
            S.op("act", lambda e: e.activation(out=sq_junk[:], in_=xt[:], func=AF.Square,
                                               accum_out=stat[:, par:par + 1]), [B_xt], [B_sqj, B_st])
            rstd_from_ss(stat[:, par:par + 1], B_st, D, stat[:, 4 + par:5 + par], B_st,
                         stat[:, 2 + par:3 + par], B_st)
            S.op("act", lambda e: e.activation(out=nb_[:], in_=xt[:], func=AF.Copy,
                                               scale=stat[:, 4 + par:5 + par]), [B_xt, B_st], [B_nb])
            for k in range(8):
                S.op("pe", lambda e, k=k: e.transpose(out=tp_[:, k * 128:(k + 1) * 128],
                                                      in_=nb_[:, k * 128:(k + 1) * 128], identity=ident[:]),
                     [B_nb, B_ident], [B_tp])
            S.op("act", lambda e: e.activation(out=nT[:, :, col * 128:(col + 1) * 128],
                                               in_=tp_[:].rearrange("p (k t) -> p k t", k=8), func=AF.Copy),
                 [B_tp], [B_nT])

        tcount = [0]

        nbq = min(4, NS)
        for bq in range(NS // nbq):
            for i in range(nbq):
                t = bq * nbq + i
                norm_tile(x_own[t * 128:(t + 1) * 128, :], tcount[0] % NXS, tcount[0] % 2, i)
                tcount[0] += 1
            ncol = nbq * 128
            c0 = bq * ncol
            for h in range(4):
                m_, B_m = next_mm()
                for k in range(8):
                    S.op("pe", lambda e, k=k, h=h, m_=m_: e.matmul(m_[:, 0:ncol], lhsT=w_in_bf[:, k, h * 128:(h + 1) * 128],
                                                                  rhs=nT[:, k, 0:ncol], start=(k == 0), stop=(k == 7)),
                         [B_win, B_nT], [B_m])
                S.op("act", lambda e, h=h, m_=m_: e.activation(out=qT_own[:, h, c0:c0 + ncol], in_=m_[:, 0:ncol], func=AF.Copy),
                     [B_m], [B_qT])
            for ct in range(4):
                m_, B_m = next_mm()
                for k in range(8):
                    S.op("pe", lambda e, k=k, ct=ct, m_=m_: e.matmul(m_[:, 0:ncol], lhsT=w_in_bf[:, k, 2048 + ct * 128:2048 + (ct + 1) * 128],
                                                                    rhs=nT[:, k, 0:ncol], start=(k == 0), stop=(k == 7)),
                         [B_win, B_nT], [B_m])
                gelu_tanh(m_[:, 0:ncol], B_m, mixT_lru[:, ct, c0:c0 + ncol], B_mlru,
                          ra[:, 0, 0:ncol], B_ra, ib[:, 0, 0:ncol], B_ib)

        def a_norm(g, i):
            T = g * 4 + i
            norm_tile(x_full[T * 128:(T + 1) * 128, :], tcount[0] % NXS, tcount[0] % 2, i)
            tcount[0] += 1

        def a_kproj(g):
            for h in range(4):
                m_, B_m = next_mm()
                for k in range(8):
                    S.op("pe", lambda e, k=k, h=h, m_=m_: e.matmul(m_[:], lhsT=w_in_bf[:, k, 512 + h * 128:512 + (h + 1) * 128],
                                                                  rhs=nT[:, k, :], start=(k == 0), stop=(k == 7)),
                         [B_win, B_nT], [B_m])
                S.op("act", lambda e, h=h, m_=m_: e.activation(out=kT_sb[:, h, :], in_=m_[:], func=AF.Copy), [B_m], [B_kTsb])
            S.dma("sp", kT_d.rearrange("h p s -> p h s")[:, :, g * 512:(g + 1) * 512], kT_sb[:],
                  reads=[B_kTsb], writes=[B_scr["kT"]], acc=True)

        def a_vproj(g):
            for i in range(4):
                m_, B_m = next_mm()
                for k in range(8):
                    S.op("pe", lambda e, k=k, i=i, m_=m_: e.matmul(m_[:], lhsT=nT[:, k, i * 128:(i + 1) * 128],
                                                                  rhs=w_in_bf[:, k, 1024:1536], start=(k == 0), stop=(k == 7)),
                         [B_win, B_nT], [B_m])
                S.op("act", lambda e, i=i, m_=m_: e.activation(out=v_sb[:, :, i, :], in_=m_[:].rearrange("p (h e) -> p h e", h=4),
                                                              func=AF.Copy), [B_m], [B_vsb])
            S.dma("sp", v_d.rearrange("h p b e -> p h b e")[:, :, g * 4:(g + 1) * 4, :], v_sb[:],
                  reads=[B_vsb], writes=[B_scr["v"]], acc=True)

        def a_xrproj(g):
            for ct in range(4):
                m_, B_m = next_mm()
                for k in range(8):
                    S.op("pe", lambda e, k=k, ct=ct, m_=m_: e.matmul(m_[:], lhsT=w_in_bf[:, k, 1536 + ct * 128:1536 + (ct + 1) * 128],
                                                                    rhs=nT[:, k, :], start=(k == 0), stop=(k == 7)),
                         [B_win, B_nT], [B_m])
                S.op("act", lambda e, ct=ct, m_=m_: e.activation(out=xrbuf[:, ct, 3:515], in_=m_[:], func=AF.Copy), [B_m], [B_xr])

        def a_conv(g):
            for ct in range(4):
                cw = lambda tap, ct=ct: pvec[:, PV_CW + ct * 4 + tap:PV_CW + ct * 4 + tap + 1]
                S.op("act", lambda e, ct=ct, cw=cw: e.activation(out=ybuf[:, ct, :], in_=xrbuf[:, ct, 3:515], func=AF.Identity,
                                                                scale=cw(3), bias=pvec[:, PV_CB + ct:PV_CB + ct + 1]),
                     [B_xr, B_pvec], [B_y])
                for tap in range(3):
                    S.op("dve", lambda e, ct=ct, tap=tap, cw=cw: e.scalar_tensor_tensor(
                        out=ybuf[:, ct, :], in0=xrbuf[:, ct, tap:tap + 512], scalar=cw(tap), in1=ybuf[:, ct, :],
                        op0=ALU.mult, op1=ALU.add), [B_xr, B_pvec, B_y], [B_y])
            S.op("dve", lambda e: e.tensor_copy(out=xrbuf[:, :, 0:3], in_=xrbuf[:, :, 512:515]), [B_xr], [B_xr])
            S.op("act", lambda e: e.activation(out=ybf[:], in_=ybuf[:], func=AF.Copy), [B_y], [B_ybf])

        def a_gates(g, cts):
            for ct in cts:
                for a, (dst, B_dst, bcol) in enumerate(((ra, B_ra, PV_BA), (ib, B_ib, PV_BX))):
                    g_, B_g = gp[a]
                    S.op("pe", lambda e, a=a, ct=ct, g_=g_: e.matmul(g_[:], lhsT=wbd[:, a, ct, :], rhs=ybf[:, ct, :],
                                                                    start=True, stop=True), [B_wbd, B_ybf], [B_g])
                    S.op("act", lambda e, ct=ct, g_=g_, dst=dst, bcol=bcol: e.activation(
                        out=dst[:, ct, :], in_=g_[:], func=AF.Sigmoid, bias=pvec[:, bcol + ct:bcol + ct + 1]),
                        [B_g, B_pvec], [B_dst])

        def a_exp(g):
            for ct in range(4):
                S.op("act", lambda e, ct=ct: e.activation(out=ra[:, ct, :], in_=ra[:, ct, :], func=AF.Exp,
                                                         scale=consts[:, 1 + ct:2 + ct]), [B_ra, B_consts], [B_ra])
            S.op("dve", lambda e: e.scalar_tensor_tensor(out=hb[:], in0=ra[:], scalar=-1.0, in1=ra[:],
                                                         op0=ALU.mult, op1=ALU.mult), [B_ra], [B_hb])
            S.op("dve", lambda e: e.tensor_scalar(out=hb[:], in0=hb[:], scalar1=1.0, scalar2=1e-12,
                                                  op0=ALU.add, op1=ALU.max), [B_hb], [B_hb])

        def a_sqrt(g):
            S.op("act", lambda e: e.activation(out=hb[:], in_=hb[:], func=AF.Sqrt), [B_hb], [B_hb])
            S.op("dve", lambda e: e.tensor_tensor(out=ib[:], in0=ib[:], in1=hb[:], op=ALU.mult), [B_ib, B_hb], [B_ib])
            S.op("dve", lambda e: e.tensor_tensor(out=ib[:], in0=ib[:], in1=ybuf[:], op=ALU.mult), [B_ib, B_y], [B_ib])

        def a_scan(g):
            for ct in range(4):
                S.op("dve", lambda e, ct=ct: e.tensor_tensor_scan(out=hb[:, ct, :], data0=ra[:, ct, :], data1=ib[:, ct, :],
                                                                 initial=carry[:, ct:ct + 1], op0=ALU.mult, op1=ALU.add),
                     [B_ra, B_ib, B_carry], [B_hb])
            S.op("dve", lambda e: e.tensor_copy(out=carry[:], in_=hb[:, :, 511]), [B_hb], [B_carry])

        def a_select(g):
            for i in range(4):
                col = (g % 2) * 4 + i
                if col == 0:
                    S.op("dve", lambda e, i=i, col=col: e.tensor_scalar(out=selacc[:], in0=hb[:, :, i * 128:(i + 1) * 128],
                                                                       scalar1=selw[:, col:col + 1], scalar2=None, op0=ALU.mult),
                         [B_hb, B_selw], [B_selacc])
                else:
                    S.op("dve", lambda e, i=i, col=col: e.scalar_tensor_tensor(
                        out=selacc[:], in0=hb[:, :, i * 128:(i + 1) * 128], scalar=selw[:, col:col + 1], in1=selacc[:],
                        op0=ALU.mult, op1=ALU.add), [B_hb, B_selw, B_selacc], [B_selacc])
            if g % 2 == 1:
                j = g // 2
                S.op("dve", lambda e, j=j: e.tensor_tensor(out=mixT_lru[:, :, j * 128:(j + 1) * 128], in0=selacc[:],
                                                          in1=mixT_lru[:, :, j * 128:(j + 1) * 128], op=ALU.mult),
                     [B_selacc, B_mlru], [B_mlru])

        for i in range(4):
            a_norm(0, i)
        a_kproj(0)
        a_vproj(0)
        a_xrproj(0)
        for g in range(G):
            nxt = g + 1 < G
            table_conversion_step(-(-len(conv_jobs) // max(1, G - 2 - g)) if g < G - 2 else len(conv_jobs))
            a_conv(g)
            if nxt:
                a_norm(g + 1, 0)
            a_gates(g, (0, 1))
            if nxt:
                a_norm(g + 1, 1)
            a_gates(g, (2, 3))
            if nxt:
                a_norm(g + 1, 2)
            a_exp(g)
            if nxt:
                a_norm(g + 1, 3)
            a_sqrt(g)
            if nxt:
                a_kproj(g + 1)
            a_scan(g)
            if nxt:
                a_vproj(g + 1)
            a_select(g)
            if nxt:
                a_xrproj(g + 1)
        S.barrier()
    if debug:
        S.dma("sp", dbg["qT"][:, :, :], qT_own[:], reads=[B_qT], writes=[B_dbg], acc=True)
        S.dma("sp", dbg["lru"][:, :, :], mixT_lru[:], reads=[B_mlru], writes=[B_dbg], acc=True)
    if stop_after == "A":
        S.bg.clear()
        S.barrier()
        return nc, S


    with ExitStack() as pb:
        NKS = 4
        kblk = [sb(pb, f"kblk{i}", [128, 512], BF16) for i in range(NKS)]
        vblk = [sb(pb, f"vblk{i}", [128, 4, 130], BF16) for i in range(NKS)]
        E = [[sb(pb, f"E{m}_{i}", [128, 4, 128], BF16) for i in range(2)] for m in range(2)]
        o1s, B_o1s = sb(pb, "o1s", [128, 128], F32)
        osb, B_osb = sb(pb, "osb", [128, 128], F32)
        sqj, B_sqjB = sb(pb, "sqjB", [128, 128], F32)
        attn_out, B_attn = sb(pb, "attn_out", [128, 512], BF16)
        fstat, B_fstat = sb(pb, "fstat", [128, 8], F32)
        sc = [[ps(pb, f"sc{m}_{i}", [128, 512], F32) for i in range(2)] for m in range(2)]
        O = [ps(pb, f"O{m}", [128, 512], F32) for m in range(2)]
        tpB, B_tpB = ps(pb, "tpB", [128, 1024], BF16)
        for i in range(NKS):
            S.op("dve", lambda e, i=i: e.memset(vblk[i][0][:, :, 128:130], 1.0), [], [vblk[i][1]])
        steps = [(j, h, c) for j in range(NS) for h in range(4) for c in range(2 * j + 2)]

        def load_kv(n):
            j, h, c = steps[n]
            kb_, B_kb = kblk[n % NKS]
            vb_, B_vb = vblk[n % NKS]
            S.dma("sp", kb_[:], kT_d[h, :, c * 512:(c + 1) * 512], reads=[B_scr["kT"]], writes=[B_kb])
            S.dma("sp", vb_[:, :, 0:128], v_d[h, :, 4 * c:4 * c + 4, :], reads=[B_scr["v"]], writes=[B_vb])

        for n in range(min(3, len(steps))):
            load_kv(n)

        def qk_exp(n):
            j, h, c = steps[n]
            kb_, B_kb = kblk[n % NKS]
            es_ = n % 2
            for i in range(4):
                for m in range(2):
                    sc_, B_sc = sc[m][es_]
                    S.op("pe", lambda e, m=m, i=i, sc_=sc_, kb_=kb_, h=h, j=j: e.matmul(
                        sc_[:, i * 128:(i + 1) * 128], lhsT=kb_[64 * m:64 * m + 64, i * 128:(i + 1) * 128],
                        rhs=qT_own[64 * m:64 * m + 64, h, j * 128:(j + 1) * 128], start=True, stop=True),
                        [B_kb, B_qT], [B_sc])
            for m in range(2):
                sc_, B_sc = sc[m][es_]
                E_, B_E = E[m][es_]
                S.op("act", lambda e, sc_=sc_, E_=E_: e.activation(out=E_[:].rearrange("p a b -> p (a b)"), in_=sc_[:],
                                                                  func=AF.Exp, scale=0.125), [B_sc], [B_E])
                if c >= 2 * j:
                    r0 = (c - 2 * j) * 4
                    S.op("pool", lambda e, E_=E_, r0=r0: e.tensor_tensor(out=E_[:], in0=E_[:], in1=tmask[:, r0:r0 + 4, :],
                                                                        op=ALU.mult), [B_E, B_tmask], [B_E])

        qk_exp(0)
        for n, (j, h, c) in enumerate(steps):
            if n + 3 < len(steps):
                load_kv(n + 3)
            if n + 1 < len(steps):
                qk_exp(n + 1)
            vb_, B_vb = vblk[n % NKS]
            es_ = n % 2
            nch = 2 * j + 2
            for i in range(4):
                for m in range(2):
                    E_, B_E = E[m][es_]
                    O_, B_O = O[m]
                    S.op("pe", lambda e, m=m, i=i, E_=E_, O_=O_, vb_=vb_, c=c, nch=nch: e.matmul(
                        O_[:, 0:129], lhsT=E_[:, i, :], rhs=vb_[:, i, 0:129],
                        start=(c == 0 and i == 0), stop=(c == nch - 1 and i == 3)), [B_E, B_vb], [B_O])
            if c == nch - 1:
                (O0, B_O0), (O1, B_O1) = O
                S.op("dve", lambda e: e.reciprocal(out=fstat[:, 0:1], in_=O0[:, 128:129]), [B_O0], [B_fstat])
                S.op("dve", lambda e: e.reciprocal(out=fstat[:, 1:2], in_=O1[:, 128:129]), [B_O1], [B_fstat])
                S.op("dve", lambda e: e.tensor_scalar(out=o1s[:], in0=O1[:, 0:128], scalar1=fstat[:, 1:2], scalar2=NEGLAM,
                                                      op0=ALU.mult, op1=ALU.mult), [B_O1, B_fstat, B_consts], [B_o1s])
                S.op("dve", lambda e: e.scalar_tensor_tensor(out=osb[:], in0=O0[:, 0:128], scalar=fstat[:, 0:1], in1=o1s[:],
                                                             op0=ALU.mult, op1=ALU.add), [B_O0, B_fstat, B_o1s], [B_osb])
                S.op("act", lambda e: e.activation(out=sqj[:], in_=osb[:], func=AF.Square, accum_out=fstat[:, 2:3]),
                     [B_osb], [B_sqjB, B_fstat])
                rstd_from_ss(fstat[:, 2:3], B_fstat, 128, fstat[:, 4:5], B_fstat, fstat[:, 3:4], B_fstat)
                S.op("dve", lambda e, h=h: e.scalar_tensor_tensor(out=attn_out[:, h * 128:(h + 1) * 128], in0=osb[:],
                                                                  scalar=fstat[:, 4:5], in1=sgl[:], op0=ALU.mult, op1=ALU.mult),
                     [B_osb, B_fstat, B_sgl], [B_attn])
                if h == 3:
                    for hh in range(4):
                        S.op("pe", lambda e, hh=hh: e.transpose(out=tpB[:, hh * 128:(hh + 1) * 128],
                                                                in_=attn_out[:, hh * 128:(hh + 1) * 128], identity=ident[:]),
                             [B_attn, B_ident], [B_tpB])
                    S.op("act", lambda e, j=j: e.activation(out=mixT_attn[:, :, j * 128:(j + 1) * 128],
                                                           in_=tpB[:, 0:512].rearrange("p (h t) -> p h t", h=4), func=AF.Copy),
                         [B_tpB], [B_mattn])
        S.barrier()
    pq.close()
    if debug:
        S.dma("sp", dbg["attn"][:, :, :], mixT_attn[:], reads=[B_mattn], writes=[B_dbg], acc=True)
    if stop_after == "B":
        S.bg.clear()
        S.barrier()
        return nc, S
    B_out = Buf("out_own")
    S.bg.discard(B_tabs)
    with ExitStack() as pc:
        w_out_bf, B_wout = sb(pc, "w_out_bf", [128, 8, D], BF16)
        wq_bf, B_wq = sb(pc, "wq_bf", [128, 8, 2048], BF16)
        skT, B_skT = sb(pc, "skT", [128, 2, 128], BF16)
        io16, B_io16 = sb(pc, "io16", [128, 16], F32)
        xo, B_xo = sb(pc, "xo", [128, D], F32)
        h1s = [sb(pc, "h1_0", [128, D], F32)] * 2
        xnb, B_xnb = sb(pc, "xnb", [128, D], BF16)
        junk, B_junk = sb(pc, "junkC", [128, D], BF16)
        junk2, B_junk2 = sb(pc, "junkC2", [128, D], BF16)
        xnT, B_xnT = sb(pc, "xnT", [128, 8, 128], BF16)
        qT16, B_qT16 = sb(pc, "qT16", [128, 16, 128], BF16)
        cst, B_cst = sb(pc, "cst", [128, 8], F32)
        s2m, _ = sb(pc, "s2m", [128, 4, 128], F32)
        B_s2m = [Buf(f"s2m{i}") for i in range(4)]
        c2m, _ = sb(pc, "c2m", [128, 4, 256], F32)
        B_c2m = [Buf(f"c2m{i}") for i in range(4)]
        B_vvc = [Buf(f"vvc{i}") for i in range(4)]
        B_ixc = [Buf(f"ixc{i}") for i in range(4)]
        B_scvc = [Buf(f"scvc{i}") for i in range(4)]
        B_cic = [Buf(f"cic{i}") for i in range(4)]
        vv, B_vv = sb(pc, "vv", [128, 16, 16], F32)
        ix, B_ix = sb(pc, "ix", [128, 16, 16], U32)
        ixf, B_ixf = sb(pc, "ixf", [128, 16, 16], F32)
        cand, B_cand = sb(pc, "cand", [128, 4, 256], F32)
        scv, B_scv = sb(pc, "scv", [128, 8, 16], F32)
        ci, B_ci = sb(pc, "ci", [128, 8, 16], U32)
        cih, B_cih = sb(pc, "cih", [128, 8, 16], U32)
        cil, B_cil = sb(pc, "cil", [128, 8, 16], U32)
        cihf, B_cihf = sb(pc, "cihf", [128, 8, 16], F32)
        cilf, B_cilf = sb(pc, "cilf", [128, 8, 16], F32)
        oh, B_oh = sb(pc, "oh", [128, 4, 16, 16], F32)
        oh2, B_oh2 = oh, B_oh
        e1, B_e1 = sb(pc, "e1", [128, 8, 16], F32)
        e2, B_e2 = sb(pc, "e2", [128, 8, 16], F32)
        eidf, B_eidf = sb(pc, "eidf", [128, 128], F32)
        eids = [sb(pc, f"eid{i}", [128, 128], U32) for i in range(2)]
        gw, B_gw = sb(pc, "gw", [128, 8, 16], F32)
        den, B_den = sb(pc, "den", [128, 8], F32)
        actv, B_actv = sb(pc, "actv", [128, 128], F32)
        B_actc = [Buf(f"actc{i}") for i in range(4)]
        gt1, B_gt1 = sb(pc, "gt1", [128, 128], F32)
        gt2, B_gt2 = sb(pc, "gt2", [128, 128], F32)
        wsl, B_wsl = sb(pc, "wsl", [128, 128], F32)
        NGB = 12
        GB = [sb(pc, f"GB{i}", [128, 2 * D], BF16) for i in range(NGB)]
        ubf = [sb(pc, f"ubf{i}", [128, D], BF16) for i in range(2)]
        osb2, B_osb2 = xo, B_xo
        hp, B_hp = ps(pc, "hp", [128, 1024], F32)
        tpC, B_tpC = ps(pc, "tpC", [128, 1024], BF16)
        qps, B_qps = ps(pc, "qps", [128, 512], F32)
        sps = [ps(pc, f"sps{i}", [128, 512], F32) for i in range(2)]
        pp, B_pp = ps(pc, "pp", [128, 1024], F32)
        G2 = bvec[:, BV_G2:BV_G2 + D]
        GF = bvec[:, BV_GF:BV_GF + D]

        with ExitStack() as pw0:
            load_weight_bf(pw0, w_out_bf, B_wout, w_out, D, nm="wout")
            S.barrier()
        with ExitStack() as pw:
            load_weight_bf(pw, wq_bf, B_wq, w_query, 2048, nm="wq")
            skf, B_skf = sb(pw, "skf", [128, 2, 128], F32)
            skb, B_skb = sb(pw, "skb", [128, 2, 128], BF16)
            S.dma("sp", skf[:], sk.rearrange("a k d -> k a d"), writes=[B_skf])
            S.op("dve", lambda e: e.tensor_copy(out=skb[:], in_=skf[:]), [B_skf], [B_skb])
            for a in range(2):
                S.op("pe", lambda e, a=a: e.transpose(out=tpC[:, a * 128:(a + 1) * 128], in_=skb[:, a, :], identity=ident[:]),
                     [B_skb, B_ident], [B_tpC])
            S.op("act", lambda e: e.activation(out=skT[:], in_=tpC[:, 0:256].rearrange("p (a k) -> p a k", a=2), func=AF.Copy),
                 [B_tpC], [B_skT])
            S.op("pool", lambda e: e.iota(io16[:], pattern=[[1, 16]], base=0, channel_multiplier=0,
                                          allow_small_or_imprecise_dtypes=True), [], [B_io16])
            S.barrier()

        def top16(src_ap, B_src, work, B_work, val_ap, B_val, idx_ap, B_idx):
            S.op("dve", lambda e: e.max(out=val_ap[:, 0:8], in_=src_ap), [B_src], [B_val])
            S.op("dve", lambda e: e.match_replace(out=work, in_to_replace=val_ap[:, 0:8], in_values=src_ap, imm_value=-1e30),
                 [B_src, B_val], [B_work])
            S.op("dve", lambda e: e.max(out=val_ap[:, 8:16], in_=work), [B_work], [B_val])
            S.op("dve", lambda e: e.max_index(out=idx_ap[:, 0:8], in_max=val_ap[:, 0:8], in_values=src_ap),
                 [B_src, B_val], [B_idx])
            S.op("dve", lambda e: e.max_index(out=idx_ap[:, 8:16], in_max=val_ap[:, 8:16], in_values=work),
                 [B_work, B_val], [B_idx])

        def S1a(j):
            jc = slice(j * 128, (j + 1) * 128)
            h1, B_h1 = h1s[j % 2]
            S.dma("sp", xo[:], x_own[j * 128:(j + 1) * 128, :], writes=[B_xo])
            for half in range(2):
                for ch in range(8):
                    src, B_src = (mixT_attn, B_mattn) if ch < 4 else (mixT_lru, B_mlru)
                    S.op("pe", lambda e, half=half, ch=ch, src=src: e.matmul(
                        hp[:, half * 512:(half + 1) * 512], lhsT=src[:, ch % 4, jc], rhs=w_out_bf[:, ch, half * 512:(half + 1) * 512],
                        start=(ch == 0), stop=(ch == 7)), [B_src, B_wout], [B_hp])
            S.op("dve", lambda e: e.tensor_tensor(out=h1[:], in0=hp[:], in1=xo[:], op=ALU.add), [B_hp, B_xo], [B_h1])
            if debug:
                S.dma("sp", dbg["h1"][j * 128:(j + 1) * 128, :], h1[:], reads=[B_h1], writes=[B_dbg], acc=True)
            S.op("act", lambda e: e.activation(out=junk2[:], in_=h1[:], func=AF.Square, accum_out=cst[:, 0:1]), [B_h1], [B_junk2, B_cst])
            rstd_from_ss(cst[:, 0:1], B_cst, D, cst[:, 2:3], B_cst, cst[:, 1:2], B_cst)
            S.op("dve", lambda e: e.scalar_tensor_tensor(out=hp[:], in0=h1[:], scalar=cst[:, 2:3], in1=G2, op0=ALU.mult, op1=ALU.mult),
                 [B_h1, B_cst, B_bvec], [B_hp])
            S.op("act", lambda e: e.activation(out=xnb[:], in_=hp[:], func=AF.Copy), [B_hp], [B_xnb])
            for k in range(8):
                S.op("pe", lambda e, k=k: e.transpose(out=tpC[:, k * 128:(k + 1) * 128], in_=xnb[:, k * 128:(k + 1) * 128],
                                                      identity=ident[:]), [B_xnb, B_ident], [B_tpC])
            S.op("act", lambda e: e.activation(out=xnT[:], in_=tpC[:].rearrange("p (k t) -> p k t", k=8), func=AF.Copy),
                 [B_tpC], [B_xnT])
            for q4 in range(4):
                for a in range(4):
                    hh = q4 * 4 + a
                    for k in range(8):
                        S.op("pe", lambda e, a=a, hh=hh, k=k: e.matmul(qps[:, a * 128:(a + 1) * 128],
                                                                      lhsT=wq_bf[:, k, hh * 128:(hh + 1) * 128], rhs=xnT[:, k, :],
                                                                      start=(k == 0), stop=(k == 7)), [B_wq, B_xnT], [B_qps])
                S.op("act", lambda e, q4=q4: e.activation(out=qT16[:, q4 * 4:(q4 + 1) * 4, :].rearrange("p a t -> p (a t)"), in_=qps[:],
                                                         func=AF.Copy), [B_qps], [B_qT16])

        def top16_multi(srcs, B_src, works, B_work, vals, B_val, idxs, B_idx):
            n = len(srcs)
            for c_ in range(n):
                S.op("dve", lambda e, c_=c_: e.max(out=vals[c_][:, 0:8], in_=srcs[c_]), [B_src], [B_val[c_]])
            for c_ in range(n):
                S.op("dve", lambda e, c_=c_: e.match_replace(out=works[c_], in_to_replace=vals[c_][:, 0:8], in_values=srcs[c_],
                                                            imm_value=-1e30), [B_src, B_val[c_]], [B_work[c_]])
            for c_ in range(n):
                S.op("dve", lambda e, c_=c_: e.max(out=vals[c_][:, 8:16], in_=works[c_]), [B_work[c_]], [B_val[c_]])
            for c_ in range(n):
                S.op("dve", lambda e, c_=c_: e.max_index(out=idxs[c_][:, 0:8], in_max=vals[c_][:, 0:8], in_values=srcs[c_]),
                     [B_src, B_val[c_]], [B_idx[c_]])
            for c_ in range(n):
                S.op("dve", lambda e, c_=c_: e.max_index(out=idxs[c_][:, 8:16], in_max=vals[c_][:, 8:16], in_values=works[c_]),
                     [B_work[c_], B_val[c_]], [B_idx[c_]])

        def S1b(j):
            eid, B_eid = eids[j % 2]
            for q4 in range(4):
                sp_, B_sp = sps[q4 % 2]
                for a in range(4):
                    hh = q4 * 4 + a
                    S.op("pe", lambda e, a=a, hh=hh, sp_=sp_: e.matmul(sp_[:, a * 128:(a + 1) * 128], lhsT=qT16[:, hh, :],
                                                                      rhs=skT[:, hh % 2, :], start=True, stop=True),
                         [B_qT16, B_skT], [B_sp])
                top16_multi([sp_[:, a * 128:(a + 1) * 128] for a in range(4)], B_sp,
                            [s2m[:, a, :] for a in range(4)], B_s2m,
                            [vv[:, q4 * 4 + a, :] for a in range(4)], B_vvc,
                            [ix[:, q4 * 4 + a, :] for a in range(4)], B_ixc)
            vv4 = vv[:].rearrange("p (h a) k -> p h a k", a=2)
            for hq in range(2):
                hs = slice(hq * 4, hq * 4 + 4)
                S.op("dve", lambda e, hs=hs: e.tensor_tensor(
                    out=cand[:].rearrange("p h (a b) -> p h a b", a=16),
                    in0=vv4[:, hs, 0, :].unsqueeze(3).broadcast_to([128, 4, 16, 16]),
                    in1=vv4[:, hs, 1, :].unsqueeze(2).broadcast_to([128, 4, 16, 16]), op=ALU.add), B_vvc, [B_cand])
                top16_multi([cand[:, a, :] for a in range(4)], B_cand,
                            [c2m[:, a, :] for a in range(4)], B_c2m,
                            [scv[:, hq * 4 + a, :] for a in range(4)], B_scvc,
                            [ci[:, hq * 4 + a, :] for a in range(4)], B_cic)
            S.op("dve", lambda e: e.tensor_single_scalar(out=cih[:], in_=ci[:], scalar=4, op=ALU.logical_shift_right), B_cic, [B_cih])
            S.op("dve", lambda e: e.tensor_single_scalar(out=cil[:], in_=ci[:], scalar=15, op=ALU.bitwise_and), B_cic, [B_cil])
            S.op("dve", lambda e: e.tensor_copy(out=ixf[:], in_=ix[:]), B_ixc, [B_ixf])
            S.op("dve", lambda e: e.tensor_copy(out=cihf[:], in_=cih[:]), [B_cih], [B_cihf])
            S.op("dve", lambda e: e.tensor_copy(out=cilf[:], in_=cil[:]), [B_cil], [B_cilf])
            ixf4 = ixf[:].rearrange("p (h a) k -> p h a k", a=2)
            io_b = io16[:].unsqueeze(1).unsqueeze(1).broadcast_to([128, 4, 16, 16])
            for (cf, B_cf, a, eo, B_eo) in ((cihf, B_cihf, 0, e1, B_e1), (cilf, B_cilf, 1, e2, B_e2)):
                for hq in range(2):
                    hs = slice(hq * 4, hq * 4 + 4)
                    S.op("dve", lambda e, cf=cf, hs=hs: e.tensor_tensor(
                        out=oh[:], in0=cf[:, hs, :].unsqueeze(3).broadcast_to([128, 4, 16, 16]), in1=io_b, op=ALU.is_equal),
                        [B_cf, B_io16], [B_oh])
                    S.op("dve", lambda e, a=a, hs=hs: e.tensor_tensor(
                        out=oh[:], in0=oh[:], in1=ixf4[:, hs, a, :].unsqueeze(2).broadcast_to([128, 4, 16, 16]), op=ALU.mult),
                        [B_oh, B_ixf], [B_oh])
                    S.op("dve", lambda e, eo=eo, hs=hs: e.tensor_reduce(out=eo[:, hs, :], in_=oh[:], axis=mybir.AxisListType.X, op=ALU.add),
                         [B_oh], [B_eo])
            S.op("dve", lambda e: e.scalar_tensor_tensor(out=eidf[:], in0=e1[:].rearrange("p h k -> p (h k)"), scalar=128.0,
                                                         in1=e2[:].rearrange("p h k -> p (h k)"), op0=ALU.mult, op1=ALU.add),
                 [B_e1, B_e2], [B_eidf])
            S.op("dve", lambda e: e.tensor_copy(out=eid[:], in_=eidf[:]), [B_eidf], [B_eid])
            if debug:
                S.dma("sp", dbg["eid"][j * 128:(j + 1) * 128, :], eid[:], reads=[B_eid], writes=[B_dbg], acc=True)

        gcnt = [0]

        def S3pre(j):
            S.op("dve", lambda e: e.tensor_tensor(out=gw[:], in0=scv[:], in1=scv[:, :, 0:1].broadcast_to([128, 8, 16]),
                                                  op=ALU.subtract), B_scvc, [B_gw])
            S.op("act", lambda e: e.activation(out=gw[:], in_=gw[:], func=AF.Exp), [B_gw], [B_gw])
            S.op("dve", lambda e: e.tensor_reduce(out=den[:], in_=gw[:], axis=mybir.AxisListType.X, op=ALU.add), [B_gw], [B_den])
            S.op("dve", lambda e: e.reciprocal(out=den[:], in_=den[:]), [B_den], [B_den])
            S.op("dve", lambda e: e.tensor_tensor(out=gw[:], in0=gw[:], in1=den[:].unsqueeze(2).broadcast_to([128, 8, 16]),
                                                  op=ALU.mult), [B_gw, B_den], [B_gw])

        GRP = 4
        gwf = gw[:].rearrange("p h k -> p (h k)")

        def peer_group(j, q):
            eid, B_eid = eids[j % 2]
            cs = slice(q * GRP, (q + 1) * GRP)
            used = []
            for a in range(GRP):
                s_ = q * GRP + a
                g_, B_g = GB[gcnt[0] % NGB]
                gcnt[0] += 1
                used.append((g_, B_g))
                S.dma("pool", None, None, reads=[B_eid, B_tabs], writes=[B_g],
                      fn=lambda e, g_=g_, s_=s_: e.indirect_dma_start(out=g_[:], out_offset=None, in_=e_cat[:, :],
                                                                    in_offset=bass.IndirectOffsetOnAxis(ap=eid[:, s_:s_ + 1], axis=0)))
                S.op("dve", lambda e, g_=g_, s_=s_: e.scalar_tensor_tensor(out=g_[:, 0:D], in0=g_[:, 0:D], scalar=1.0, in1=hp[:],
                                                                          op0=ALU.mult, op1=ALU.mult, accum_out=actv[:, s_:s_ + 1]),
                     [B_g, B_hp], [B_g, B_actc[a]])
            S.op("dve", lambda e: e.tensor_tensor(out=gt1[:, cs], in0=actv[:, cs], in1=actv[:, cs], op=ALU.mult), B_actc, [B_gt1])
            S.op("dve", lambda e: e.tensor_scalar(out=gt1[:, cs], in0=gt1[:, cs], scalar1=0.044715, scalar2=1.0,
                                                  op0=ALU.mult, op1=ALU.add), [B_gt1], [B_gt1])
            S.op("dve", lambda e: e.tensor_tensor(out=gt1[:, cs], in0=gt1[:, cs], in1=actv[:, cs], op=ALU.mult), [B_gt1] + B_actc, [B_gt1])
            S.op("act", lambda e: e.activation(out=gt2[:, cs], in_=gt1[:, cs], func=AF.Sigmoid, scale=GELU_C), [B_gt1], [B_gt2])
            S.op("dve", lambda e: e.tensor_tensor(out=gt2[:, cs], in0=gt2[:, cs], in1=actv[:, cs], op=ALU.mult), [B_gt2] + B_actc, [B_gt2])
            S.op("dve", lambda e: e.tensor_tensor(out=wsl[:, cs], in0=gt2[:, cs], in1=gwf[:, cs], op=ALU.mult), [B_gt2, B_gw], [B_wsl])
            for a in range(GRP):
                s_ = q * GRP + a
                g_, B_g = used[a]
                ub_, B_ub = ubf[s_ % 2]
                S.op("act", lambda e, g_=g_, ub_=ub_, s_=s_: e.activation(out=ub_[:], in_=g_[:, D:2 * D], func=AF.Copy,
                                                                         scale=wsl[:, s_:s_ + 1]), [B_g, B_wsl], [B_ub])
                for half in range(2):
                    S.op("pe", lambda e, half=half, ub_=ub_, s_=s_: e.matmul(pp[:, half * 512:(half + 1) * 512], lhsT=ident[:],
                                                                            rhs=ub_[:, half * 512:(half + 1) * 512],
                                                                            start=(s_ == 0), stop=(s_ == 127)), [B_ub, B_ident], [B_pp])

        def S5(j):
            h1, B_h1 = h1s[j % 2]
            S.op("dve", lambda e: e.tensor_tensor(out=h1[:], in0=pp[:], in1=h1[:], op=ALU.add), [B_pp, B_h1], [B_h1])
            S.op("act", lambda e: e.activation(out=junk2[:], in_=h1[:], func=AF.Square, accum_out=cst[:, 4:5]), [B_h1], [B_junk2, B_cst])
            rstd_from_ss(cst[:, 4:5], B_cst, D, cst[:, 6:7], B_cst, cst[:, 5:6], B_cst)
            S.op("dve", lambda e: e.scalar_tensor_tensor(out=osb2[:], in0=h1[:], scalar=cst[:, 6:7], in1=GF, op0=ALU.mult, op1=ALU.mult),
                 [B_h1, B_cst, B_bvec], [B_osb2])
            S.dma("sp", out_d[j * 128:(j + 1) * 128, :], osb2[:], reads=[B_osb2], writes=[B_out], acc=True)

        for j in range(NS):
            S1a(j)
            S1b(j)
            S3pre(j)
            for q in range(128 // GRP):
                peer_group(j, q)
            S5(j)
        S.barrier()
    S.final_wait("sp", [B_out, B_dbg])
    return nc, S


def make_in_maps(inp, S_LEN):
    f = lambda a: np.ascontiguousarray(np.asarray(a, dtype=np.float32))
    NB = S_LEN // 128
    NS = NB // NCORES
    x = f(inp["x"]).reshape(S_LEN, D)
    pvec = np.zeros((128, PV_N), np.float32)
    pvec[:, PV_G1:PV_G1 + 8] = f(inp["norm1_g"])[0].reshape(8, 128).T
    cw = f(inp["conv_w"])[0]
    pvec[:, PV_CW:PV_CW + 16] = cw.reshape(4, 4, 128).transpose(2, 1, 0).reshape(128, 16)
    pvec[:, PV_CB:PV_CB + 4] = f(inp["conv_b"])[0].reshape(4, 128).T
    pvec[:, PV_BA:PV_BA + 4] = f(inp["b_rec_gate"])[0].reshape(4, 128).T
    pvec[:, PV_BX:PV_BX + 4] = f(inp["b_in_gate"])[0].reshape(4, 128).T
    pvec[:, PV_LAM:PV_LAM + 4] = f(inp["lru_lambda"])[0].reshape(4, 128).T
    bvec = np.concatenate([f(inp["norm2_g"])[0], f(inp["norm_f_g"]), f(inp["subln_g"])[0],
                           f(inp["lambda_q1"])[0], f(inp["lambda_k1"])[0],
                           f(inp["lambda_q2"])[0], f(inp["lambda_k2"])[0]]).reshape(1, BV_N)
    wgate = np.stack([f(inp["w_rec_gate"])[0], f(inp["w_in_gate"])[0]])
    skk = np.stack([f(inp["sub_keys_1"])[0], f(inp["sub_keys_2"])[0]])
    common = {
        "x_full": x, "w_in": f(inp["w_in"])[0], "w_out": f(inp["w_out"])[0], "w_query": f(inp["w_query"])[0],
        "sk": skk, "wgate": wgate, "expert_down": f(inp["expert_down"])[0], "expert_up": f(inp["expert_up"])[0],
        "pvec": pvec, "bvec": bvec,
    }
    diag = np.ones((128, 128), np.float32)
    diag[64:, :64] = 0.0
    maps = []
    for c in range(NCORES):
        xb = x.reshape(NB, 128, D)
        x_own = np.ascontiguousarray(xb[c::8].reshape(NS * 128, D))
        tm = np.zeros((128, 8, 128), np.float32)
        for r in range(8):
            if r < c:
                tm[:, r, :] = 1.0
            elif r == c:
                tm[:, r, :] = diag
        selw = np.zeros((128, 8), np.float32)
        selw[:, c] = 1.0
        m = dict(common)
        m.update({"x_own": x_own, "tmask": tm, "selw": selw})
        maps.append(m)
    return maps


_CACHE = {}


def kernel(**inputs):
    S_LEN = int(np.asarray(inputs["x"]).shape[1])
    NB = S_LEN // 128
    NS = NB // NCORES
    if S_LEN not in _CACHE:
        _CACHE[S_LEN] = build(S_LEN)
    nc, _ = _CACHE[S_LEN]
    maps = make_in_maps(inputs, S_LEN)
    res = run_bass_kernel_spmd(nc, maps, core_ids=list(range(NCORES)))
    out = np.zeros((NB, 128, D), np.float32)
    for c in range(NCORES):
        out[c::8] = np.asarray(res.results[c]["out_own"], dtype=np.float32).reshape(NS, 128, D)
    return out.reshape(1, S_LEN, D)
```

```python
import numpy as np
from contextlib import ExitStack
import concourse.bass as bass
import concourse.mybir as mybir
from concourse.bass_utils import run_bass_kernel_spmd

F32 = mybir.dt.float32
BF16 = mybir.dt.bfloat16
U32 = mybir.dt.uint32
AF = mybir.ActivationFunctionType
ALU = mybir.AluOpType

D = 1024
NCORES = 8
EPS = 1e-6
EPOCH = 12000
NEXP = 16384
GELU_C = 1.5957691216057308


class Buf:
    __slots__ = ("name", "w", "r", "dsem", "dcnt", "ew")

    def __init__(self, name):
        self.name = name
        self.ew = None
        self.w = None
        self.r = []
        self.dsem = None
        self.dcnt = 0


class Sched:
    COMPUTE = ("pe", "act", "dve", "pool")

    def __init__(self, nc, es):
        self.nc = nc
        self.es = es
        self.eng = {"pe": nc.tensor, "act": nc.scalar, "dve": nc.vector, "pool": nc.gpsimd, "sp": nc.sync}
        self.cnt = {e: 0 for e in self.eng}
        self.esems = {e: [] for e in self.eng}
        self.known = {e: {} for e in self.eng}
        self.dsems = {}
        self.dbufs = []
        self.nsem = 0
        self.ninst = {e: 0 for e in self.eng}
        self.bg = set()

    def _newsem(self, name):
        self.nsem += 1
        return self.es.enter_context(self.nc.semaphore(name))

    def _etok_sem(self, e, idx):
        ep = (idx - 1) // EPOCH
        while len(self.esems[e]) <= ep:
            self.esems[e].append(self._newsem(f"c_{e}_{len(self.esems[e])}"))
        return self.esems[e][ep], (idx - 1) % EPOCH + 1, (e, ep)

    def _wait(self, e, tok):
        if tok is None:
            return
        if tok[0] == "e":
            sem, val, key = self._etok_sem(tok[1], tok[2])
        else:
            sem, val, key = self.dsems[tok[1]], tok[2], tok[1]
        if self.known[e].get(key, 0) >= val:
            return
        self.known[e][key] = val
        self.eng[e].wait_ge(sem, val)
        self.ninst[e] += 1

    STRICT = True

    def deps(self, e, reads, writes):
        for b in reads:
            self._wait(e, b.w)
        for b in writes:
            strict = self.STRICT and e != "pe"
            if b.w is not None and (strict or not (b.w[0] == "e" and b.w[1] == e)):
                self._wait(e, b.w)
            for t in b.r:
                if t[0] == "e" and t[1] == e:
                    continue
                self._wait(e, t)

    def _mark(self, tok, reads, writes):
        for b in reads:
            b.r.append(tok)
            if len(b.r) > 48:
                last = {}
                for t in b.r:
                    last[(t[0], t[1])] = t
                b.r = list(last.values())
        for b in writes:
            b.w = tok
            b.ew = tok
            b.r = []

    def op(self, e, fn, reads=(), writes=()):
        self.deps(e, reads, writes)
        inst = fn(self.eng[e])
        self.cnt[e] += 1
        idx = self.cnt[e]
        sem, val, key = self._etok_sem(e, idx)
        inst.then_inc(sem, 1)
        self.ninst[e] += 1
        tok = ("e", e, idx)
        self._mark(tok, reads, writes)
        return tok

    def dma(self, q, out, in_, reads=(), writes=(), acc=False, fn=None, **kw):
        wb = writes[0]
        if acc:
            for b in reads:
                self._wait(q, b.w)
            for b in writes:
                if b.w is not None and b.w[0] == "e":
                    self._wait(q, b.w)
                elif b.ew is not None:
                    self._wait(q, b.ew)
                for t in b.r:
                    self._wait(q, t)
        else:
            self.deps(q, reads, writes)
        if wb.dsem is None:
            wb.dsem = "d_" + wb.name
            self.dsems[wb.dsem] = self._newsem(wb.dsem)
            self.dbufs.append(wb)
        wb.dcnt += 16
        sem = self.dsems[wb.dsem]
        if fn is not None:
            inst = fn(self.eng[q])
        else:
            inst = self.eng[q].dma_start(out=out, in_=in_, **kw)
        inst.then_inc(sem, 16)
        self.ninst[q] += 1
        tok = ("d", wb.dsem, wb.dcnt)
        for b in reads:
            b.r.append(tok)
            if len(b.r) > 48:
                last = {}
                for t in b.r:
                    last[(t[0], t[1])] = t
                b.r = list(last.values())
        for b in writes:
            b.w = tok
            if not acc:
                b.r = []
        return tok

    def barrier(self):
        toks = []
        for e in self.COMPUTE:
            if self.cnt[e] > 0:
                toks.append(("e", e, self.cnt[e]))
        for b in self.dbufs:
            if b in self.bg:
                continue
            toks.append(("d", b.dsem, b.dcnt))
        for e in self.eng:
            for t in toks:
                self._wait(e, t)

    def final_wait(self, e, bufs):
        for b in bufs:
            self._wait(e, b.w)


PV_G1 = 0
PV_CW = 8
PV_CB = 24
PV_BA = 28
PV_BX = 32
PV_LAM = 36
PV_N = 40
BV_G2 = 0
BV_GF = 1024
BV_SG = 2048
BV_LQ1 = 2176
BV_LK1 = 2240
BV_LQ2 = 2304
BV_LK2 = 2368
BV_N = 2432


def build(S_LEN, debug=False, stop_after=None):
    NB = S_LEN // 128
    NS = NB // NCORES
    G = S_LEN // 512
    assert NB % 8 == 0 and G * 4 == NB and G % 2 == 0
    nc = bass.Bass("TRN2", target_bir_lowering=False)
    top = ExitStack()
    S = Sched(nc, top)

    def dram(name, shape, dt, kind="ExternalInput"):
        return nc.dram_tensor(name, shape, dt, kind=kind).ap()

    x_full = dram("x_full", [S_LEN, D], F32)
    x_own = dram("x_own", [NS * 128, D], F32)
    w_in = dram("w_in", [D, 2560], F32)
    w_out = dram("w_out", [D, D], F32)
    w_query = dram("w_query", [D, 2048], F32)
    sk = dram("sk", [2, 128, 128], F32)
    wgate = dram("wgate", [2, 8, 64, 64], F32)
    e_down = dram("expert_down", [NEXP, D], F32)
    e_up = dram("expert_up", [NEXP, D], F32)
    pvec_d = dram("pvec", [128, PV_N], F32)
    bvec_d = dram("bvec", [1, BV_N], F32)
    tmask_d = dram("tmask", [128, 8, 128], F32)
    selw_d = dram("selw", [128, 8], F32)
    out_d = dram("out_own", [NS * 128, D], F32, kind="ExternalOutput")
    skind = "ExternalOutput" if debug else "Internal"
    kT_d = dram("kT_scr", [4, 128, S_LEN], BF16, kind=skind)
    v_d = dram("v_scr", [4, 128, NB, 128], BF16, kind=skind)
    dbg = {}
    if debug:
        dbg["qT"] = dram("dbg_qT", [128, 4, NS * 128], BF16, kind="ExternalOutput")
        dbg["lru"] = dram("dbg_lru", [128, 4, NS * 128], BF16, kind="ExternalOutput")
        dbg["attn"] = dram("dbg_attn", [128, 4, NS * 128], BF16, kind="ExternalOutput")
        dbg["h1"] = dram("dbg_h1", [NS * 128, D], F32, kind="ExternalOutput")
        dbg["eid"] = dram("dbg_eid", [NS * 128, 128], U32, kind="ExternalOutput")
        dbg["g"] = dram("dbg_g", [NS * 128, 128], F32, kind="ExternalOutput")
        dbg["act"] = dram("dbg_act", [NS * 128, 128], F32, kind="ExternalOutput")
    B_dbg = Buf("dbgout")
    B_scr = {"kT": Buf("kTscr"), "v": Buf("vscr")}
    e_cat = dram("e_cat_bf", [NEXP, 2 * D], BF16, kind="Internal")
    B_tabs = Buf("etabs")
    S.bg.add(B_tabs)

    CR = 512
    conv_jobs = [(half, r) for r in range(NEXP // CR) for half in range(2)]

    def table_conversion_step(n):
        for _ in range(n):
            if not conv_jobs:
                return
            half, r = conv_jobs.pop(0)
            src_t = (e_down, e_up)[half]
            S.dma("pool", e_cat[r * CR:(r + 1) * CR, half * D:(half + 1) * D], src_t[r * CR:(r + 1) * CR, :],
                  writes=[B_tabs], acc=True)

    def sb(es, name, shape, dt):
        return es.enter_context(nc.sbuf_tensor(name, shape, dt)), Buf(name)

    def ps(es, name, shape, dt):
        return es.enter_context(nc.psum_tensor(name, shape, dt)), Buf(name)

    ident, B_ident = sb(top, "ident", [128, 128], BF16)
    pvec, B_pvec = sb(top, "pvec_sb", [128, PV_N], F32)
    bvec, B_bvec = sb(top, "bvec_sb", [128, BV_N], F32)
    tmask, B_tmask = sb(top, "tmask_bf", [128, 8, 128], BF16)
    selw, B_selw = sb(top, "selw_sb", [128, 8], F32)
    consts, B_consts = sb(top, "consts", [128, 16], F32)
    mixT_lru, B_mlru = sb(top, "mixT_lru", [128, 4, NS * 128], BF16)
    mixT_attn, B_mattn = sb(top, "mixT_attn", [128, 4, NS * 128], BF16)
    sgl, B_sgl = sb(top, "sgl", [128, 128], F32)
    pq = ExitStack()
    qT_own, B_qT = sb(pq, "qT_own", [128, 4, NS * 128], BF16)
    NEGH = consts[:, 0:1]

    def rstd_from_ss(ss_ap, B_ss, n, out_ap, B_out, tmp_ap, B_tmp):
        S.op("dve", lambda e: e.tensor_scalar(out=tmp_ap, in0=ss_ap, scalar1=1.0 / n, scalar2=EPS,
                                              op0=ALU.mult, op1=ALU.add), [B_ss], [B_tmp])
        k = tmp_ap.shape[1] if len(tmp_ap.shape) > 1 else 1
        nh = NEGH if k == 1 else NEGH.broadcast_to([128, k])
        S.op("pool", lambda e: e.tensor_tensor(out=out_ap, in0=tmp_ap, in1=nh, op=ALU.pow),
             [B_tmp, B_consts], [B_out])

    with ExitStack() as p0:
        ioc, B_ioc = sb(p0, "ioc", [128, 128], F32)
        iop, B_iop = sb(p0, "iop", [128, 128], F32)
        tmf, B_tmf = sb(p0, "tmf", [128, 8, 128], F32)
        sc0, B_sc0 = sb(p0, "sc0", [128, 64], F32)
        sm0, B_sm0 = sb(p0, "sm0", [128, 8], F32)
        S.dma("sp", pvec[:], pvec_d[:, :], writes=[B_pvec])
        S.dma("sp", bvec[:], bvec_d[0, :].partition_broadcast(128), writes=[B_bvec])
        S.dma("sp", tmf[:], tmask_d[:, :, :], writes=[B_tmf])
        S.dma("sp", selw[:], selw_d[:, :], writes=[B_selw])
        S.op("pool", lambda e: e.iota(ioc[:], pattern=[[1, 128]], base=0, channel_multiplier=0,
                                      allow_small_or_imprecise_dtypes=True), [], [B_ioc])
        S.op("pool", lambda e: e.iota(iop[:], pattern=[[0, 128]], base=0, channel_multiplier=1,
                                      allow_small_or_imprecise_dtypes=True), [], [B_iop])
        S.op("dve", lambda e: e.tensor_tensor(out=ident[:], in0=ioc[:], in1=iop[:], op=ALU.is_equal),
             [B_ioc, B_iop], [B_ident])
        S.op("dve", lambda e: e.tensor_copy(out=tmask[:], in_=tmf[:]), [B_tmf], [B_tmask])
        S.op("pool", lambda e: e.memset(consts[:, 0:1], -0.5), [], [B_consts])
        S.op("act", lambda e: e.activation(out=sm0[:, 0:4], in_=pvec[:, PV_LAM:PV_LAM + 4], func=AF.Exp, scale=-1.0),
             [B_pvec], [B_sm0])
        S.op("act", lambda e: e.activation(out=sm0[:, 0:4], in_=sm0[:, 0:4], func=AF.Ln, bias=1.0),
             [B_sm0], [B_sm0])
        S.op("dve", lambda e: e.tensor_scalar(out=consts[:, 1:5], in0=sm0[:, 0:4], scalar1=-8.0, scalar2=None,
                                              op0=ALU.mult), [B_sm0], [B_consts])
        S.op("dve", lambda e: e.scalar_tensor_tensor(out=sc0[:], in0=bvec[:, BV_LQ1:BV_LQ1 + 64], scalar=1.0,
                                                     in1=bvec[:, BV_LK1:BV_LK1 + 64],
                                                     op0=ALU.mult, op1=ALU.mult, accum_out=sm0[:, 4:5]),
             [B_bvec], [B_sc0, B_sm0])
        S.op("dve", lambda e: e.scalar_tensor_tensor(out=sc0[:], in0=bvec[:, BV_LQ2:BV_LQ2 + 64], scalar=1.0,
                                                     in1=bvec[:, BV_LK2:BV_LK2 + 64],
                                                     op0=ALU.mult, op1=ALU.mult, accum_out=sm0[:, 5:6]),
             [B_bvec], [B_sc0, B_sm0])
        S.op("act", lambda e: e.activation(out=sm0[:, 6:8], in_=sm0[:, 4:6], func=AF.Exp), [B_sm0], [B_sm0])
        S.op("dve", lambda e: e.tensor_tensor(out=sm0[:, 4:5], in0=sm0[:, 7:8], in1=sm0[:, 6:7], op=ALU.subtract),
             [B_sm0], [B_sm0])
        S.op("dve", lambda e: e.tensor_scalar(out=consts[:, 5:6], in0=sm0[:, 4:5], scalar1=-0.2, scalar2=None,
                                              op0=ALU.add), [B_sm0], [B_consts])
        S.op("dve", lambda e: e.tensor_scalar(out=sgl[:], in0=bvec[:, BV_SG:BV_SG + 128], scalar1=0.8, scalar2=None,
                                              op0=ALU.mult), [B_bvec], [B_sgl])
        S.barrier()
    NEGLAM = consts[:, 5:6]

    def load_weight_bf(es_stage, dst, B_dst, src_ap, ncols, scale_col=None, nm="w"):
        stg = []
        for i in range(2):
            stg.append(sb(es_stage, f"stg_{nm}{i}", [128, ncols], F32))
        for k in range(8):
            t, B_t = stg[k % 2]
            S.dma("sp", t[:], src_ap[k * 128:(k + 1) * 128, :], writes=[B_t])
            if scale_col is not None:
                S.op("dve", lambda e, t=t, k=k: e.tensor_scalar(out=dst[:, k, :], in0=t[:],
                                                               scalar1=pvec[:, scale_col + k:scale_col + k + 1],
                                                               scalar2=None, op0=ALU.mult), [B_t, B_pvec], [B_dst])
            else:
                S.op("act", lambda e, t=t, k=k: e.activation(out=dst[:, k, :], in_=t[:], func=AF.Copy),
                     [B_t], [B_dst])

    def gelu_tanh(x_ap, B_x, out_ap, B_out, t1, B_t1, t2, B_t2):
        S.op("act", lambda e: e.activation(out=t1, in_=x_ap, func=AF.Square), [B_x], [B_t1])
        S.op("dve", lambda e: e.tensor_scalar(out=t1, in0=t1, scalar1=0.044715, scalar2=1.0,
                                              op0=ALU.mult, op1=ALU.add), [B_t1], [B_t1])
        S.op("dve", lambda e: e.tensor_tensor(out=t2, in0=t1, in1=x_ap, op=ALU.mult), [B_t1, B_x], [B_t2])
        S.op("act", lambda e: e.activation(out=t2, in_=t2, func=AF.Sigmoid, scale=GELU_C), [B_t2], [B_t2])
        S.op("dve", lambda e: e.tensor_tensor(out=out_ap, in0=t2, in1=x_ap, op=ALU.mult), [B_t2, B_x], [B_out])

    with ExitStack() as pa:
        w_in_bf, B_win = sb(pa, "w_in_bf", [128, 8, 2560], BF16)
        wbd, B_wbd = sb(pa, "wbd", [128, 2, 4, 128], BF16)
        NXS = 2
        xts = [sb(pa, f"xt{i}", [128, D], F32) for i in range(NXS)]
        nbf = [sb(pa, f"nbf{i}", [128, D], BF16) for i in range(2)]
        sq_junk, B_sqj = sb(pa, "sq_junk", [128, D], BF16)
        nT, B_nT = sb(pa, "nT", [128, 8, 512], BF16)
        stat, B_stat = sb(pa, "stat", [128, 8], F32)
        B_stats = [Buf("stat0"), Buf("stat1")]
        kT_sb, B_kTsb = sb(pa, "kT_sb", [128, 4, 512], BF16)
        v_sb, B_vsb = sb(pa, "v_sb", [128, 4, 4, 128], BF16)
        xrbuf, B_xr = sb(pa, "xrbuf", [128, 4, 515], F32)
        ybuf, B_y = sb(pa, "ybuf", [128, 4, 512], F32)
        ybf, B_ybf = sb(pa, "ybf", [128, 4, 512], BF16)
        ra, B_ra = sb(pa, "ra", [128, 4, 512], F32)
        ib, B_ib = sb(pa, "ib", [128, 4, 512], F32)
        hb, B_hb = sb(pa, "hb", [128, 4, 512], F32)
        carry, B_carry = sb(pa, "carry", [128, 4], F32)
        selacc, B_selacc = sb(pa, "selacc", [128, 4, 128], F32)
        tp = [ps(pa, f"tpA{i}", [128, 1024], BF16) for i in range(2)]
        mm = [ps(pa, f"mmA{i}", [128, 512], F32) for i in range(3)]
        gp = [ps(pa, f"gpA{i}", [128, 512], F32) for i in range(2)]
        mmi = [0]

        def next_mm():
            mmi[0] += 1
            return mm[mmi[0] % 3]

        with ExitStack() as pw:
            load_weight_bf(pw, w_in_bf, B_win, w_in, 2560, scale_col=PV_G1, nm="win")
            wbf, B_wbf = sb(pw, "wbf", [128, 2, 4, 128], F32)
            S.op("pool", lambda e: e.memset(wbf[:], 0.0), [], [B_wbf])
            for a in range(2):
                for ct in range(4):
                    for hf in range(2):
                        S.dma("sp", wbf[hf * 64:(hf + 1) * 64, a, ct, hf * 64:(hf + 1) * 64],
                              wgate[a, 2 * ct + hf, :, :], writes=[B_wbf], acc=True)
            S.op("dve", lambda e: e.tensor_copy(out=wbd[:], in_=wbf[:]), [B_wbf], [B_wbd])
            S.op("dve", lambda e: e.memset(xrbuf[:], 0.0), [], [B_xr])
            S.op("dve", lambda e: e.memset(carry[:], 0.0), [], [B_carry])
            S.barrier()

        def norm_tile(src_ap, slot, par, col):
            xt, B_xt = xts[slot]
            nb_, B_nb = nbf[par]
            tp_, B_tp = tp[par]
            B_st = B_stats[par]
            S.dma("sp", xt[:], src_ap, writes=[B_xt])
            S.op("act", lambda e: e.activation(out=sq_junk[:], in_=xt[:], func=AF.Square,
                                               accum_out=stat[:, par:par + 1]), [B_xt], [B_sqj, B_st])
            rstd_from_ss(stat[:, par:par + 1], B_st, D, stat[:, 4 + par:5 + par], B_st,
                         stat[:, 2 + par:3 + par], B_st)
            S.op("act", lambda e: e.activation(out=nb_[:], in_=xt[:], func=AF.Copy,
                                               scale=stat[:, 4 + par:5 + par]), [B_xt, B_st], [B_nb])
            for k in range(8):
                S.op("pe", lambda e, k=k: e.transpose(out=tp_[:, k * 128:(k + 1) * 128],
                                                      in_=nb_[:, k * 128:(k + 1) * 128], identity=ident[:]),
                     [B_nb, B_ident], [B_tp])
            S.op("act", lambda e: e.activation(out=nT[:, :, col * 128:(col + 1) * 128],
                                               in_=tp_[:].rearrange("p (k t) -> p k t", k=8), func=AF.Copy),
                 [B_tp], [B_nT])

        tcount = [0]

        nbq = min(4, NS)
        for bq in range(NS // nbq):
            for i in range(nbq):
                t = bq * nbq + i
                norm_tile(x_own[t * 128:(t + 1) * 128, :], tcount[0] % NXS, tcount[0] % 2, i)
                tcount[0] += 1
            ncol = nbq * 128
            c0 = bq * ncol
            for h in range(4):
                m_, B_m = next_mm()
                for k in range(8):
                    S.op("pe", lambda e, k=k, h=h, m_=m_: e.matmul(m_[:, 0:ncol], lhsT=w_in_bf[:, k, h * 128:(h + 1) * 128],
                                                                  rhs=nT[:, k, 0:ncol], start=(k == 0), stop=(k == 7)),
                         [B_win, B_nT], [B_m])
                S.op("act", lambda e, h=h, m_=m_: e.activation(out=qT_own[:, h, c0:c0 + ncol], in_=m_[:, 0:ncol], func=AF.Copy),
                     [B_m], [B_qT])
            for ct in range(4):
                m_, B_m = next_mm()
                for k in range(8):
                    S.op("pe", lambda e, k=k, ct=ct, m_=m_: e.matmul(m_[:, 0:ncol], lhsT=w_in_bf[:, k, 2048 + ct * 128:2048 + (ct + 1) * 128],
                                                                    rhs=nT[:, k, 0:ncol], start=(k == 0), stop=(k == 7)),
                         [B_win, B_nT], [B_m])
                gelu_tanh(m_[:, 0:ncol], B_m, mixT_lru[:, ct, c0:c0 + ncol], B_mlru,
                          ra[:, 0, 0:ncol], B_ra, ib[:, 0, 0:ncol], B_ib)

        def a_norm(g, i):
            T = g * 4 + i
            norm_tile(x_full[T * 128:(T + 1) * 128, :], tcount[0] % NXS, tcount[0] % 2, i)
            tcount[0] += 1

        def a_kproj(g):
            for h in range(4):
                m_, B_m = next_mm()
                for k in range(8):
                    S.op("pe", lambda e, k=k, h=h, m_=m_: e.matmul(m_[:], lhsT=w_in_bf[:, k, 512 + h * 128:512 + (h + 1) * 128],
                                                                  rhs=nT[:, k, :], start=(k == 0), stop=(k == 7)),
                         [B_win, B_nT], [B_m])
                S.op("act", lambda e, h=h, m_=m_: e.activation(out=kT_sb[:, h, :], in_=m_[:], func=AF.Copy), [B_m], [B_kTsb])
            S.dma("sp", kT_d.rearrange("h p s -> p h s")[:, :, g * 512:(g + 1) * 512], kT_sb[:],
                  reads=[B_kTsb], writes=[B_scr["kT"]], acc=True)

        def a_vproj(g):
            for i in range(4):
                m_, B_m = next_mm()
                for k in range(8):
                    S.op("pe", lambda e, k=k, i=i, m_=m_: e.matmul(m_[:], lhsT=nT[:, k, i * 128:(i + 1) * 128],
                                                                  rhs=w_in_bf[:, k, 1024:1536], start=(k == 0), stop=(k == 7)),
                         [B_win, B_nT], [B_m])
                S.op("act", lambda e, i=i, m_=m_: e.activation(out=v_sb[:, :, i, :], in_=m_[:].rearrange("p (h e) -> p h e", h=4),
                                                              func=AF.Copy), [B_m], [B_vsb])
            S.dma("sp", v_d.rearrange("h p b e -> p h b e")[:, :, g * 4:(g + 1) * 4, :], v_sb[:],
                  reads=[B_vsb], writes=[B_scr["v"]], acc=True)

        def a_xrproj(g):
            for ct in range(4):
                m_, B_m = next_mm()
                for k in range(8):
                    S.op("pe", lambda e, k=k, ct=ct, m_=m_: e.matmul(m_[:], lhsT=w_in_bf[:, k, 1536 + ct * 128:1536 + (ct + 1) * 128],
                                                                    rhs=nT[:, k, :], start=(k == 0), stop=(k == 7)),
                         [B_win, B_nT], [B_m])
                S.op("act", lambda e, ct=ct, m_=m_: e.activation(out=xrbuf[:, ct, 3:515], in_=m_[:], func=AF.Copy), [B_m], [B_xr])

        def a_conv(g):
            for ct in range(4):
                cw = lambda tap, ct=ct: pvec[:, PV_CW + ct * 4 + tap:PV_CW + ct * 4 + tap + 1]
                S.op("act", lambda e, ct=ct, cw=cw: e.activation(out=ybuf[:, ct, :], in_=xrbuf[:, ct, 3:515], func=AF.Identity,
                                                                scale=cw(3), bias=pvec[:, PV_CB + ct:PV_CB + ct + 1]),
                     [B_xr, B_pvec], [B_y])
            for ct in range(4):
                cw = lambda tap, ct=ct: pvec[:, PV_CW + ct * 4 + tap:PV_CW + ct * 4 + tap + 1]
                for tap in range(3):
                    S.op("dve", lambda e, ct=ct, tap=tap, cw=cw: e.scalar_tensor_tensor(
                        out=ybuf[:, ct, :], in0=xrbuf[:, ct, tap:tap + 512], scalar=cw(tap), in1=ybuf[:, ct, :],
                        op0=ALU.mult, op1=ALU.add), [B_xr, B_pvec, B_y], [B_y])
            S.op("dve", lambda e: e.tensor_copy(out=xrbuf[:, :, 0:3], in_=xrbuf[:, :, 512:515]), [B_xr], [B_xr])
            S.op("act", lambda e: e.activation(out=ybf[:], in_=ybuf[:], func=AF.Copy), [B_y], [B_ybf])

        def a_gates(g, cts):
            for ct in cts:
                for a, (dst, B_dst, bcol) in enumerate(((ra, B_ra, PV_BA), (ib, B_ib, PV_BX))):
                    g_, B_g = gp[a]
                    S.op("pe", lambda e, a=a, ct=ct, g_=g_: e.matmul(g_[:], lhsT=wbd[:, a, ct, :], rhs=ybf[:, ct, :],
                                                                    start=True, stop=True), [B_wbd, B_ybf], [B_g])
                    S.op("act", lambda e, ct=ct, g_=g_, dst=dst, bcol=bcol: e.activation(
                        out=dst[:, ct, :], in_=g_[:], func=AF.Sigmoid, bias=pvec[:, bcol + ct:bcol + ct + 1]),
                        [B_g, B_pvec], [B_dst])

        def a_exp(g):
            for ct in range(4):
                S.op("act", lambda e, ct=ct: e.activation(out=ra[:, ct, :], in_=ra[:, ct, :], func=AF.Exp,
                                                         scale=consts[:, 1 + ct:2 + ct]), [B_ra, B_consts], [B_ra])
            S.op("dve", lambda e: e.scalar_tensor_tensor(out=hb[:], in0=ra[:], scalar=-1.0, in1=ra[:],
                                                         op0=ALU.mult, op1=ALU.mult), [B_ra], [B_hb])
            S.op("dve", lambda e: e.tensor_scalar(out=hb[:], in0=hb[:], scalar1=1.0, scalar2=1e-12,
                                                  op0=ALU.add, op1=ALU.max), [B_hb], [B_hb])

        def a_sqrt(g):
            S.op("act", lambda e: e.activation(out=hb[:], in_=hb[:], func=AF.Sqrt), [B_hb], [B_hb])
            S.op("dve", lambda e: e.tensor_tensor(out=ib[:], in0=ib[:], in1=hb[:], op=ALU.mult), [B_ib, B_hb], [B_ib])
            S.op("dve", lambda e: e.tensor_tensor(out=ib[:], in0=ib[:], in1=ybuf[:], op=ALU.mult), [B_ib, B_y], [B_ib])

        def a_scan(g):
            for ct in range(4):
                S.op("dve", lambda e, ct=ct: e.tensor_tensor_scan(out=hb[:, ct, :], data0=ra[:, ct, :], data1=ib[:, ct, :],
                                                                 initial=carry[:, ct:ct + 1], op0=ALU.mult, op1=ALU.add),
                     [B_ra, B_ib, B_carry], [B_hb])
            S.op("dve", lambda e: e.tensor_copy(out=carry[:], in_=hb[:, :, 511]), [B_hb], [B_carry])

        def a_select(g):
            for i in range(4):
                col = (g % 2) * 4 + i
                if col == 0:
                    S.op("dve", lambda e, i=i, col=col: e.tensor_scalar(out=selacc[:], in0=hb[:, :, i * 128:(i + 1) * 128],
                                                                       scalar1=selw[:, col:col + 1], scalar2=None, op0=ALU.mult),
                         [B_hb, B_selw], [B_selacc])
                else:
                    S.op("dve", lambda e, i=i, col=col: e.scalar_tensor_tensor(
                        out=selacc[:], in0=hb[:, :, i * 128:(i + 1) * 128], scalar=selw[:, col:col + 1], in1=selacc[:],
                        op0=ALU.mult, op1=ALU.add), [B_hb, B_selw, B_selacc], [B_selacc])
            if g % 2 == 1:
                j = g // 2
                S.op("dve", lambda e, j=j: e.tensor_tensor(out=mixT_lru[:, :, j * 128:(j + 1) * 128], in0=selacc[:],
                                                          in1=mixT_lru[:, :, j * 128:(j + 1) * 128], op=ALU.mult),
                     [B_selacc, B_mlru], [B_mlru])

        for i in range(4):
            a_norm(0, i)
        a_kproj(0)
        a_vproj(0)
        a_xrproj(0)
        for g in range(G):
            nxt = g + 1 < G
            table_conversion_step(-(-len(conv_jobs) // max(1, G - 2 - g)) if g < G - 2 else len(conv_jobs))
            a_conv(g)
            if nxt:
                a_norm(g + 1, 0)
            a_gates(g, (0, 1))
            if nxt:
                a_norm(g + 1, 1)
            a_gates(g, (2, 3))
            if nxt:
                a_norm(g + 1, 2)
            a_exp(g)
            if nxt:
                a_norm(g + 1, 3)
            a_sqrt(g)
            if nxt:
                a_kproj(g + 1)
            a_scan(g)
            if nxt:
                a_vproj(g + 1)
            a_select(g)
            if nxt:
                a_xrproj(g + 1)
        S.barrier()
    if debug:
        S.dma("sp", dbg["qT"][:, :, :], qT_own[:], reads=[B_qT], writes=[B_dbg], acc=True)
        S.dma("sp", dbg["lru"][:, :, :], mixT_lru[:], reads=[B_mlru], writes=[B_dbg], acc=True)
    if stop_after == "A":
        S.bg.clear()
        S.barrier()
        return nc, S


    with ExitStack() as pb:
        NKS = 4
        kblk = [sb(pb, f"kblk{i}", [128, 512], BF16) for i in range(NKS)]
        vblk = [sb(pb, f"vblk{i}", [128, 4, 130], BF16) for i in range(NKS)]
        E = [[sb(pb, f"E{m}_{i}", [128, 4, 128], BF16) for i in range(2)] for m in range(2)]
        o1s, B_o1s = sb(pb, "o1s", [128, 128], F32)
        osb, B_osb = sb(pb, "osb", [128, 128], F32)
        sqj, B_sqjB = sb(pb, "sqjB", [128, 128], F32)
        attn_out, B_attn = sb(pb, "attn_out", [128, 512], BF16)
        fstat, B_fstat = sb(pb, "fstat", [128, 8], F32)
        sc = [[ps(pb, f"sc{m}_{i}", [128, 512], F32) for i in range(2)] for m in range(2)]
        O = [ps(pb, f"O{m}", [128, 512], F32) for m in range(2)]
        tpB, B_tpB = ps(pb, "tpB", [128, 1024], BF16)
        for i in range(NKS):
            S.op("dve", lambda e, i=i: e.memset(vblk[i][0][:, :, 128:130], 1.0), [], [vblk[i][1]])
        steps = [(j, h, c) for j in range(NS) for h in range(4) for c in range(2 * j + 2)]

        def load_kv(n):
            j, h, c = steps[n]
            kb_, B_kb = kblk[n % NKS]
            vb_, B_vb = vblk[n % NKS]
            S.dma("sp", kb_[:], kT_d[h, :, c * 512:(c + 1) * 512], reads=[B_scr["kT"]], writes=[B_kb])
            S.dma("sp", vb_[:, :, 0:128], v_d[h, :, 4 * c:4 * c + 4, :], reads=[B_scr["v"]], writes=[B_vb])

        for n in range(min(3, len(steps))):
            load_kv(n)

        def qk_exp(n):
            j, h, c = steps[n]
            kb_, B_kb = kblk[n % NKS]
            es_ = n % 2
            for i in range(4):
                for m in range(2):
                    sc_, B_sc = sc[m][es_]
                    S.op("pe", lambda e, m=m, i=i, sc_=sc_, kb_=kb_, h=h, j=j: e.matmul(
                        sc_[:, i * 128:(i + 1) * 128], lhsT=kb_[64 * m:64 * m + 64, i * 128:(i + 1) * 128],
                        rhs=qT_own[64 * m:64 * m + 64, h, j * 128:(j + 1) * 128], start=True, stop=True),
                        [B_kb, B_qT], [B_sc])
            for m in range(2):
                sc_, B_sc = sc[m][es_]
                E_, B_E = E[m][es_]
                S.op("act", lambda e, sc_=sc_, E_=E_: e.activation(out=E_[:].rearrange("p a b -> p (a b)"), in_=sc_[:],
                                                                  func=AF.Exp, scale=0.125), [B_sc], [B_E])
                if c >= 2 * j:
                    r0 = (c - 2 * j) * 4
                    S.op("pool", lambda e, E_=E_, r0=r0: e.tensor_tensor(out=E_[:], in0=E_[:], in1=tmask[:, r0:r0 + 4, :],
                                                                        op=ALU.mult), [B_E, B_tmask], [B_E])

        qk_exp(0)
        for n, (j, h, c) in enumerate(steps):
            if n + 3 < len(steps):
                load_kv(n + 3)
            if n + 1 < len(steps):
                qk_exp(n + 1)
            vb_, B_vb = vblk[n % NKS]
            es_ = n % 2
            nch = 2 * j + 2
            for i in range(4):
                for m in range(2):
                    E_, B_E = E[m][es_]
                    O_, B_O = O[m]
                    S.op("pe", lambda e, m=m, i=i, E_=E_, O_=O_, vb_=vb_, c=c, nch=nch: e.matmul(
                        O_[:, 0:129], lhsT=E_[:, i, :], rhs=vb_[:, i, 0:129],
                        start=(c == 0 and i == 0), stop=(c == nch - 1 and i == 3)), [B_E, B_vb], [B_O])
            if c == nch - 1:
                (O0, B_O0), (O1, B_O1) = O
                S.op("dve", lambda e: e.reciprocal(out=fstat[:, 0:1], in_=O0[:, 128:129]), [B_O0], [B_fstat])
                S.op("dve", lambda e: e.reciprocal(out=fstat[:, 1:2], in_=O1[:, 128:129]), [B_O1], [B_fstat])
                S.op("dve", lambda e: e.tensor_scalar(out=o1s[:], in0=O1[:, 0:128], scalar1=fstat[:, 1:2], scalar2=NEGLAM,
                                                      op0=ALU.mult, op1=ALU.mult), [B_O1, B_fstat, B_consts], [B_o1s])
                S.op("dve", lambda e: e.scalar_tensor_tensor(out=osb[:], in0=O0[:, 0:128], scalar=fstat[:, 0:1], in1=o1s[:],
                                                             op0=ALU.mult, op1=ALU.add), [B_O0, B_fstat, B_o1s], [B_osb])
                S.op("act", lambda e: e.activation(out=sqj[:], in_=osb[:], func=AF.Square, accum_out=fstat[:, 2:3]),
                     [B_osb], [B_sqjB, B_fstat])
                rstd_from_ss(fstat[:, 2:3], B_fstat, 128, fstat[:, 4:5], B_fstat, fstat[:, 3:4], B_fstat)
                S.op("dve", lambda e, h=h: e.scalar_tensor_tensor(out=attn_out[:, h * 128:(h + 1) * 128], in0=osb[:],
                                                                  scalar=fstat[:, 4:5], in1=sgl[:], op0=ALU.mult, op1=ALU.mult),
                     [B_osb, B_fstat, B_sgl], [B_attn])
                if h == 3:
                    for hh in range(4):
                        S.op("pe", lambda e, hh=hh: e.transpose(out=tpB[:, hh * 128:(hh + 1) * 128],
                                                                in_=attn_out[:, hh * 128:(hh + 1) * 128], identity=ident[:]),
                             [B_attn, B_ident], [B_tpB])
                    S.op("act", lambda e, j=j: e.activation(out=mixT_attn[:, :, j * 128:(j + 1) * 128],
                                                           in_=tpB[:, 0:512].rearrange("p (h t) -> p h t", h=4), func=AF.Copy),
                         [B_tpB], [B_mattn])
        S.barrier()
    pq.close()
    if debug:
        S.dma("sp", dbg["attn"][:, :, :], mixT_attn[:], reads=[B_mattn], writes=[B_dbg], acc=True)
    if stop_after == "B":
        S.bg.clear()
        S.barrier()
        return nc, S
    B_out = Buf("out_own")
    S.bg.discard(B_tabs)
    with ExitStack() as pc:
        w_out_bf, B_wout = sb(pc, "w_out_bf", [128, 8, D], BF16)
        wq_bf, B_wq = sb(pc, "wq_bf", [128, 8, 2048], BF16)
        skT, B_skT = sb(pc, "skT", [128, 2, 128], BF16)
        io16, B_io16 = sb(pc, "io16", [128, 16], F32)
        xo, B_xo = sb(pc, "xo", [128, D], F32)
        h1s = [sb(pc, "h1_0", [128, D], F32)] * 2
        xnb, B_xnb = sb(pc, "xnb", [128, D], BF16)
        junk, B_junk = sb(pc, "junkC", [128, D], BF16)
        junk2, B_junk2 = sb(pc, "junkC2", [128, D], BF16)
        xnT, B_xnT = sb(pc, "xnT", [128, 8, 128], BF16)
        qT16, B_qT16 = sb(pc, "qT16", [128, 16, 128], BF16)
        cst, B_cst = sb(pc, "cst", [128, 8], F32)
        s2m, _ = sb(pc, "s2m", [128, 4, 128], F32)
        B_s2m = [Buf(f"s2m{i}") for i in range(4)]
        c2m, _ = sb(pc, "c2m", [128, 4, 256], F32)
        B_c2m = [Buf(f"c2m{i}") for i in range(4)]
        B_vvc = [Buf(f"vvc{i}") for i in range(4)]
        B_ixc = [Buf(f"ixc{i}") for i in range(4)]
        B_scvc = [Buf(f"scvc{i}") for i in range(4)]
        B_cic = [Buf(f"cic{i}") for i in range(4)]
        vv, B_vv = sb(pc, "vv", [128, 16, 16], F32)
        ix, B_ix = sb(pc, "ix", [128, 16, 16], U32)
        ixf, B_ixf = sb(pc, "ixf", [128, 16, 16], F32)
        cand, B_cand = sb(pc, "cand", [128, 4, 256], F32)
        scv, B_scv = sb(pc, "scv", [128, 8, 16], F32)
        ci, B_ci = sb(pc, "ci", [128, 8, 16], U32)
        cih, B_cih = sb(pc, "cih", [128, 8, 16], U32)
        cil, B_cil = sb(pc, "cil", [128, 8, 16], U32)
        cihf, B_cihf = sb(pc, "cihf", [128, 8, 16], F32)
        cilf, B_cilf = sb(pc, "cilf", [128, 8, 16], F32)
        oh, B_oh = sb(pc, "oh", [128, 4, 16, 16], F32)
        oh2, B_oh2 = oh, B_oh
        e1, B_e1 = sb(pc, "e1", [128, 8, 16], F32)
        e2, B_e2 = sb(pc, "e2", [128, 8, 16], F32)
        eidf, B_eidf = sb(pc, "eidf", [128, 128], F32)
        eids = [sb(pc, f"eid{i}", [128, 128], U32) for i in range(2)]
        gw, B_gw = sb(pc, "gw", [128, 8, 16], F32)
        den, B_den = sb(pc, "den", [128, 8], F32)
        actv, B_actv = sb(pc, "actv", [128, 128], F32)
        B_actc = [Buf(f"actc{i}") for i in range(4)]
        gt1, B_gt1 = sb(pc, "gt1", [128, 128], F32)
        gt2, B_gt2 = sb(pc, "gt2", [128, 128], F32)
        wsl, B_wsl = sb(pc, "wsl", [128, 128], F32)
        NGB = 12
        GB = [sb(pc, f"GB{i}", [128, 2 * D], BF16) for i in range(NGB)]
        ubf = [sb(pc, f"ubf{i}", [128, D], BF16) for i in range(2)]
        osb2, B_osb2 = xo, B_xo
        hp, B_hp = ps(pc, "hp", [128, 1024], F32)
        tpC, B_tpC = ps(pc, "tpC", [128, 1024], BF16)
        qps, B_qps = ps(pc, "qps", [128, 512], F32)
        sps = [ps(pc, f"sps{i}", [128, 512], F32) for i in range(2)]
        pp, B_pp = ps(pc, "pp", [128, 1024], F32)
        G2 = bvec[:, BV_G2:BV_G2 + D]
        GF = bvec[:, BV_GF:BV_GF + D]

        with ExitStack() as pw0:
            load_weight_bf(pw0, w_out_bf, B_wout, w_out, D, nm="wout")
            S.barrier()
        with ExitStack() as pw:
            load_weight_bf(pw, wq_bf, B_wq, w_query, 2048, nm="wq")
            skf, B_skf = sb(pw, "skf", [128, 2, 128], F32)
            skb, B_skb = sb(pw, "skb", [128, 2, 128], BF16)
            S.dma("sp", skf[:], sk.rearrange("a k d -> k a d"), writes=[B_skf])
            S.op("dve", lambda e: e.tensor_copy(out=skb[:], in_=skf[:]), [B_skf], [B_skb])
            for a in range(2):
                S.op("pe", lambda e, a=a: e.transpose(out=tpC[:, a * 128:(a + 1) * 128], in_=skb[:, a, :], identity=ident[:]),
                     [B_skb, B_ident], [B_tpC])
            S.op("act", lambda e: e.activation(out=skT[:], in_=tpC[:, 0:256].rearrange("p (a k) -> p a k", a=2), func=AF.Copy),
                 [B_tpC], [B_skT])
            S.op("pool", lambda e: e.iota(io16[:], pattern=[[1, 16]], base=0, channel_multiplier=0,
                                          allow_small_or_imprecise_dtypes=True), [], [B_io16])
            S.barrier()

        def top16(src_ap, B_src, work, B_work, val_ap, B_val, idx_ap, B_idx):
            S.op("dve", lambda e: e.max(out=val_ap[:, 0:8], in_=src_ap), [B_src], [B_val])
            S.op("dve", lambda e: e.match_replace(out=work, in_to_replace=val_ap[:, 0:8], in_values=src_ap, imm_value=-1e30),
                 [B_src, B_val], [B_work])
            S.op("dve", lambda e: e.max(out=val_ap[:, 8:16], in_=work), [B_work], [B_val])
            S.op("dve", lambda e: e.max_index(out=idx_ap[:, 0:8], in_max=val_ap[:, 0:8], in_values=src_ap),
                 [B_src, B_val], [B_idx])
            S.op("dve", lambda e: e.max_index(out=idx_ap[:, 8:16], in_max=val_ap[:, 8:16], in_values=work),
                 [B_work, B_val], [B_idx])

        def S1a(j):
            jc = slice(j * 128, (j + 1) * 128)
            h1, B_h1 = h1s[j % 2]
            S.dma("sp", xo[:], x_own[j * 128:(j + 1) * 128, :], writes=[B_xo])
            for half in range(2):
                for ch in range(8):
                    src, B_src = (mixT_attn, B_mattn) if ch < 4 else (mixT_lru, B_mlru)
                    S.op("pe", lambda e, half=half, ch=ch, src=src: e.matmul(
                        hp[:, half * 512:(half + 1) * 512], lhsT=src[:, ch % 4, jc], rhs=w_out_bf[:, ch, half * 512:(half + 1) * 512],
                        start=(ch == 0), stop=(ch == 7)), [B_src, B_wout], [B_hp])
            S.op("dve", lambda e: e.tensor_tensor(out=h1[:], in0=hp[:], in1=xo[:], op=ALU.add), [B_hp, B_xo], [B_h1])
            if debug:
                S.dma("sp", dbg["h1"][j * 128:(j + 1) * 128, :], h1[:], reads=[B_h1], writes=[B_dbg], acc=True)
            S.op("act", lambda e: e.activation(out=junk2[:], in_=h1[:], func=AF.Square, accum_out=cst[:, 0:1]), [B_h1], [B_junk2, B_cst])
            rstd_from_ss(cst[:, 0:1], B_cst, D, cst[:, 2:3], B_cst, cst[:, 1:2], B_cst)
            S.op("dve", lambda e: e.scalar_tensor_tensor(out=hp[:], in0=h1[:], scalar=cst[:, 2:3], in1=G2, op0=ALU.mult, op1=ALU.mult),
                 [B_h1, B_cst, B_bvec], [B_hp])
            S.op("act", lambda e: e.activation(out=xnb[:], in_=hp[:], func=AF.Copy), [B_hp], [B_xnb])
            for k in range(8):
                S.op("pe", lambda e, k=k: e.transpose(out=tpC[:, k * 128:(k + 1) * 128], in_=xnb[:, k * 128:(k + 1) * 128],
                                                      identity=ident[:]), [B_xnb, B_ident], [B_tpC])
            S.op("act", lambda e: e.activation(out=xnT[:], in_=tpC[:].rearrange("p (k t) -> p k t", k=8), func=AF.Copy),
                 [B_tpC], [B_xnT])
            for q4 in range(4):
                for a in range(4):
                    hh = q4 * 4 + a
                    for k in range(8):
                        S.op("pe", lambda e, a=a, hh=hh, k=k: e.matmul(qps[:, a * 128:(a + 1) * 128],
                                                                      lhsT=wq_bf[:, k, hh * 128:(hh + 1) * 128], rhs=xnT[:, k, :],
                                                                      start=(k == 0), stop=(k == 7)), [B_wq, B_xnT], [B_qps])
                S.op("act", lambda e, q4=q4: e.activation(out=qT16[:, q4 * 4:(q4 + 1) * 4, :].rearrange("p a t -> p (a t)"), in_=qps[:],
                                                         func=AF.Copy), [B_qps], [B_qT16])

        def top16_multi(srcs, B_src, works, B_work, vals, B_val, idxs, B_idx):
            n = len(srcs)
            for c_ in range(n):
                S.op("dve", lambda e, c_=c_: e.max(out=vals[c_][:, 0:8], in_=srcs[c_]), [B_src], [B_val[c_]])
            for c_ in range(n):
                S.op("dve", lambda e, c_=c_: e.match_replace(out=works[c_], in_to_replace=vals[c_][:, 0:8], in_values=srcs[c_],
                                                            imm_value=-1e30), [B_src, B_val[c_]], [B_work[c_]])
            for c_ in range(n):
                S.op("dve", lambda e, c_=c_: e.max(out=vals[c_][:, 8:16], in_=works[c_]), [B_work[c_]], [B_val[c_]])
            for c_ in range(n):
                S.op("dve", lambda e, c_=c_: e.max_index(out=idxs[c_][:, 0:8], in_max=vals[c_][:, 0:8], in_values=srcs[c_]),
                     [B_src, B_val[c_]], [B_idx[c_]])
            for c_ in range(n):
                S.op("dve", lambda e, c_=c_: e.max_index(out=idxs[c_][:, 8:16], in_max=vals[c_][:, 8:16], in_values=works[c_]),
                     [B_work[c_], B_val[c_]], [B_idx[c_]])

        def S1b(j):
            eid, B_eid = eids[j % 2]
            for q4 in range(4):
                sp_, B_sp = sps[q4 % 2]
                for a in range(4):
                    hh = q4 * 4 + a
                    S.op("pe", lambda e, a=a, hh=hh, sp_=sp_: e.matmul(sp_[:, a * 128:(a + 1) * 128], lhsT=qT16[:, hh, :],
                                                                      rhs=skT[:, hh % 2, :], start=True, stop=True),
                         [B_qT16, B_skT], [B_sp])
                top16_multi([sp_[:, a * 128:(a + 1) * 128] for a in range(4)], B_sp,
                            [s2m[:, a, :] for a in range(4)], B_s2m,
                            [vv[:, q4 * 4 + a, :] for a in range(4)], B_vvc,
                            [ix[:, q4 * 4 + a, :] for a in range(4)], B_ixc)
            vv4 = vv[:].rearrange("p (h a) k -> p h a k", a=2)
            for hq in range(2):
                hs = slice(hq * 4, hq * 4 + 4)
                S.op("dve", lambda e, hs=hs: e.tensor_tensor(
                    out=cand[:].rearrange("p h (a b) -> p h a b", a=16),
                    in0=vv4[:, hs, 0, :].unsqueeze(3).broadcast_to([128, 4, 16, 16]),
                    in1=vv4[:, hs, 1, :].unsqueeze(2).broadcast_to([128, 4, 16, 16]), op=ALU.add), B_vvc, [B_cand])
                top16_multi([cand[:, a, :] for a in range(4)], B_cand,
                            [c2m[:, a, :] for a in range(4)], B_c2m,
                            [scv[:, hq * 4 + a, :] for a in range(4)], B_scvc,
                            [ci[:, hq * 4 + a, :] for a in range(4)], B_cic)
            S.op("dve", lambda e: e.tensor_single_scalar(out=cih[:], in_=ci[:], scalar=4, op=ALU.logical_shift_right), B_cic, [B_cih])
            S.op("dve", lambda e: e.tensor_single_scalar(out=cil[:], in_=ci[:], scalar=15, op=ALU.bitwise_and), B_cic, [B_cil])
            S.op("dve", lambda e: e.tensor_copy(out=ixf[:], in_=ix[:]), B_ixc, [B_ixf])
            S.op("dve", lambda e: e.tensor_copy(out=cihf[:], in_=cih[:]), [B_cih], [B_cihf])
            S.op("dve", lambda e: e.tensor_copy(out=cilf[:], in_=cil[:]), [B_cil], [B_cilf])
            ixf4 = ixf[:].rearrange("p (h a) k -> p h a k", a=2)
            io_b = io16[:].unsqueeze(1).unsqueeze(1).broadcast_to([128, 4, 16, 16])
            for (cf, B_cf, a, eo, B_eo) in ((cihf, B_cihf, 0, e1, B_e1), (cilf, B_cilf, 1, e2, B_e2)):
                for hq in range(2):
                    hs = slice(hq * 4, hq * 4 + 4)
                    S.op("dve", lambda e, cf=cf, hs=hs: e.tensor_tensor(
                        out=oh[:], in0=cf[:, hs, :].unsqueeze(3).broadcast_to([128, 4, 16, 16]), in1=io_b, op=ALU.is_equal),
                        [B_cf, B_io16], [B_oh])
                    S.op("dve", lambda e, a=a, hs=hs: e.tensor_tensor(
                        out=oh[:], in0=oh[:], in1=ixf4[:, hs, a, :].unsqueeze(2).broadcast_to([128, 4, 16, 16]), op=ALU.mult),
                        [B_oh, B_ixf], [B_oh])
                    S.op("dve", lambda e, eo=eo, hs=hs: e.tensor_reduce(out=eo[:, hs, :], in_=oh[:], axis=mybir.AxisListType.X, op=ALU.add),
                         [B_oh], [B_eo])
            S.op("dve", lambda e: e.scalar_tensor_tensor(out=eidf[:], in0=e1[:].rearrange("p h k -> p (h k)"), scalar=128.0,
                                                         in1=e2[:].rearrange("p h k -> p (h k)"), op0=ALU.mult, op1=ALU.add),
                 [B_e1, B_e2], [B_eidf])
            S.op("dve", lambda e: e.tensor_copy(out=eid[:], in_=eidf[:]), [B_eidf], [B_eid])
            if debug:
                S.dma("sp", dbg["eid"][j * 128:(j + 1) * 128, :], eid[:], reads=[B_eid], writes=[B_dbg], acc=True)

        gcnt = [0]

        def S3pre(j):
            S.op("dve", lambda e: e.tensor_tensor(out=gw[:], in0=scv[:], in1=scv[:, :, 0:1].broadcast_to([128, 8, 16]),
                                                  op=ALU.subtract), B_scvc, [B_gw])
            S.op("act", lambda e: e.activation(out=gw[:], in_=gw[:], func=AF.Exp), [B_gw], [B_gw])
            S.op("dve", lambda e: e.tensor_reduce(out=den[:], in_=gw[:], axis=mybir.AxisListType.X, op=ALU.add), [B_gw], [B_den])
            S.op("dve", lambda e: e.reciprocal(out=den[:], in_=den[:]), [B_den], [B_den])
            S.op("dve", lambda e: e.tensor_tensor(out=gw[:], in0=gw[:], in1=den[:].unsqueeze(2).broadcast_to([128, 8, 16]),
                                                  op=ALU.mult), [B_gw, B_den], [B_gw])

        GRP = 4
        gwf = gw[:].rearrange("p h k -> p (h k)")

        def peer_group(j, q):
            eid, B_eid = eids[j % 2]
            cs = slice(q * GRP, (q + 1) * GRP)
            used = []
            for a in range(GRP):
                s_ = q * GRP + a
                g_, B_g = GB[gcnt[0] % NGB]
                gcnt[0] += 1
                used.append((g_, B_g))
                S.dma("pool", None, None, reads=[B_eid, B_tabs], writes=[B_g],
                      fn=lambda e, g_=g_, s_=s_: e.indirect_dma_start(out=g_[:], out_offset=None, in_=e_cat[:, :],
                                                                    in_offset=bass.IndirectOffsetOnAxis(ap=eid[:, s_:s_ + 1], axis=0)))
                S.op("dve", lambda e, g_=g_, s_=s_: e.scalar_tensor_tensor(out=g_[:, 0:D], in0=g_[:, 0:D], scalar=1.0, in1=hp[:],
                                                                          op0=ALU.mult, op1=ALU.mult, accum_out=actv[:, s_:s_ + 1]),
                     [B_g, B_hp], [B_g, B_actc[a]])
            S.op("dve", lambda e: e.tensor_tensor(out=gt1[:, cs], in0=actv[:, cs], in1=actv[:, cs], op=ALU.mult), B_actc, [B_gt1])
            S.op("dve", lambda e: e.tensor_scalar(out=gt1[:, cs], in0=gt1[:, cs], scalar1=0.044715, scalar2=1.0,
                                                  op0=ALU.mult, op1=ALU.add), [B_gt1], [B_gt1])
            S.op("dve", lambda e: e.tensor_tensor(out=gt1[:, cs], in0=gt1[:, cs], in1=actv[:, cs], op=ALU.mult), [B_gt1] + B_actc, [B_gt1])
            S.op("act", lambda e: e.activation(out=gt2[:, cs], in_=gt1[:, cs], func=AF.Sigmoid, scale=GELU_C), [B_gt1], [B_gt2])
            S.op("dve", lambda e: e.tensor_tensor(out=gt2[:, cs], in0=gt2[:, cs], in1=actv[:, cs], op=ALU.mult), [B_gt2] + B_actc, [B_gt2])
            S.op("dve", lambda e: e.tensor_tensor(out=wsl[:, cs], in0=gt2[:, cs], in1=gwf[:, cs], op=ALU.mult), [B_gt2, B_gw], [B_wsl])
            for a in range(GRP):
                s_ = q * GRP + a
                g_, B_g = used[a]
                ub_, B_ub = ubf[s_ % 2]
                S.op("act", lambda e, g_=g_, ub_=ub_, s_=s_: e.activation(out=ub_[:], in_=g_[:, D:2 * D], func=AF.Copy,
                                                                         scale=wsl[:, s_:s_ + 1]), [B_g, B_wsl], [B_ub])
                for half in range(2):
                    S.op("pe", lambda e, half=half, ub_=ub_, s_=s_: e.matmul(pp[:, half * 512:(half + 1) * 512], lhsT=ident[:],
                                                                            rhs=ub_[:, half * 512:(half + 1) * 512],
                                                                            start=(s_ == 0), stop=(s_ == 127)), [B_ub, B_ident], [B_pp])

        def S5(j):
            h1, B_h1 = h1s[j % 2]
            S.op("dve", lambda e: e.tensor_tensor(out=h1[:], in0=pp[:], in1=h1[:], op=ALU.add), [B_pp, B_h1], [B_h1])
            S.op("act", lambda e: e.activation(out=junk2[:], in_=h1[:], func=AF.Square, accum_out=cst[:, 4:5]), [B_h1], [B_junk2, B_cst])
            rstd_from_ss(cst[:, 4:5], B_cst, D, cst[:, 6:7], B_cst, cst[:, 5:6], B_cst)
            S.op("dve", lambda e: e.scalar_tensor_tensor(out=osb2[:], in0=h1[:], scalar=cst[:, 6:7], in1=GF, op0=ALU.mult, op1=ALU.mult),
                 [B_h1, B_cst, B_bvec], [B_osb2])
            S.dma("sp", out_d[j * 128:(j + 1) * 128, :], osb2[:], reads=[B_osb2], writes=[B_out], acc=True)

        for j in range(NS):
            S1a(j)
            S1b(j)
            S3pre(j)
            for q in range(128 // GRP):
                peer_group(j, q)
            S5(j)
        S.barrier()
    S.final_wait("sp", [B_out, B_dbg])
    return nc, S


def make_in_maps(inp, S_LEN):
    f = lambda a: np.ascontiguousarray(np.asarray(a, dtype=np.float32))
    NB = S_LEN // 128
    NS = NB // NCORES
    x = f(inp["x"]).reshape(S_LEN, D)
    pvec = np.zeros((128, PV_N), np.float32)
    pvec[:, PV_G1:PV_G1 + 8] = f(inp["norm1_g"])[0].reshape(8, 128).T
    cw = f(inp["conv_w"])[0]
    pvec[:, PV_CW:PV_CW + 16] = cw.reshape(4, 4, 128).transpose(2, 1, 0).reshape(128, 16)
    pvec[:, PV_CB:PV_CB + 4] = f(inp["conv_b"])[0].reshape(4, 128).T
    pvec[:, PV_BA:PV_BA + 4] = f(inp["b_rec_gate"])[0].reshape(4, 128).T
    pvec[:, PV_BX:PV_BX + 4] = f(inp["b_in_gate"])[0].reshape(4, 128).T
    pvec[:, PV_LAM:PV_LAM + 4] = f(inp["lru_lambda"])[0].reshape(4, 128).T
    bvec = np.concatenate([f(inp["norm2_g"])[0], f(inp["norm_f_g"]), f(inp["subln_g"])[0],
                           f(inp["lambda_q1"])[0], f(inp["lambda_k1"])[0],
                           f(inp["lambda_q2"])[0], f(inp["lambda_k2"])[0]]).reshape(1, BV_N)
    wgate = np.stack([f(inp["w_rec_gate"])[0], f(inp["w_in_gate"])[0]])
    skk = np.stack([f(inp["sub_keys_1"])[0], f(inp["sub_keys_2"])[0]])
    common = {
        "x_full": x, "w_in": f(inp["w_in"])[0], "w_out": f(inp["w_out"])[0], "w_query": f(inp["w_query"])[0],
        "sk": skk, "wgate": wgate, "expert_down": f(inp["expert_down"])[0], "expert_up": f(inp["expert_up"])[0],
        "pvec": pvec, "bvec": bvec,
    }
    diag = np.ones((128, 128), np.float32)
    diag[64:, :64] = 0.0
    maps = []
    for c in range(NCORES):
        xb = x.reshape(NB, 128, D)
        x_own = np.ascontiguousarray(xb[c::8].reshape(NS * 128, D))
        tm = np.zeros((128, 8, 128), np.float32)
        for r in range(8):
            if r < c:
                tm[:, r, :] = 1.0
            elif r == c:
                tm[:, r, :] = diag
        selw = np.zeros((128, 8), np.float32)
        selw[:, c] = 1.0
        m = dict(common)
        m.update({"x_own": x_own, "tmask": tm, "selw": selw})
        maps.append(m)
    return maps


_CACHE = {}


def kernel(**inputs):
    S_LEN = int(np.asarray(inputs["x"]).shape[1])
    NB = S_LEN // 128
    NS = NB // NCORES
    if S_LEN not in _CACHE:
        _CACHE[S_LEN] = build(S_LEN)
    nc, _ = _CACHE[S_LEN]
    maps = make_in_maps(inputs, S_LEN)
    res = run_bass_kernel_spmd(nc, maps, core_ids=list(range(NCORES)))
    out = np.zeros((NB, 128, D), np.float32)
    for c in range(NCORES):
        out[c::8] = np.asarray(res.results[c]["out_own"], dtype=np.float32).reshape(NS, 128, D)
    return out.reshape(1, S_LEN, D)
```
